# Optimizing a Trainium2 kernel written in Bass

```python
import jax, jax.numpy as jnp
from jax import lax
import numpy as np

D_MODEL = 1024
BATCH = 8
SEQ = 8192
DEPTH = 2

CHUNK = 64
EPS = 1e-6
HEAD_DIM = 64
H_A = D_MODEL // (2 * HEAD_DIM)
A_PREV_CHUNKS = 8
MAX_REL_DIST = 256
H_B = D_MODEL // (2 * HEAD_DIM)
H_B_KV = H_B // 4
B_WINDOW = 128
B_PREV_CHUNKS = B_WINDOW // CHUNK
ATTN_PROJ = 3 * H_A * HEAD_DIM + H_B * HEAD_DIM + 2 * H_B_KV * HEAD_DIM
D_INNER = 2 * D_MODEL
SSM_HEAD_DIM = 64
SSM_HEADS = D_INNER // SSM_HEAD_DIM
SSM_GROUPS = 4
SSM_STATE = 128
SSM_CONV = 4
SSD_CHUNK = 64
SSM_CONV_CH = D_INNER + 2 * SSM_GROUPS * SSM_STATE
SSM_PROJ = D_INNER + SSM_CONV_CH + SSM_HEADS
D_FF = ((8 * D_MODEL // 3 + 127) // 128) * 128
FFN_CONV = 3

kernel_name = "chunk_causal_hybrid_attn_ssd_convffn"


def rmsnorm(x, g):
    xf = x.astype(jnp.float32)
    y = xf * lax.rsqrt(jnp.mean(xf * xf, axis=-1, keepdims=True) + EPS)
    return (y * g.astype(jnp.float32)).astype(x.dtype)


def causal_dwconv(x, w, b):
    k = w.shape[0]
    y = lax.conv_general_dilated(
        x, w[:, None, :].astype(x.dtype), window_strides=(1,), padding=[(k - 1, 0)],
        dimension_numbers=("NWC", "WIO", "NWC"), feature_group_count=x.shape[-1])
    return y + b.astype(x.dtype)


def band_offsets(n_prev):
    band = (n_prev + 1) * CHUNK
    q_off = jnp.arange(CHUNK, dtype=jnp.int32)
    k_off = jnp.arange(band, dtype=jnp.int32) - n_prev * CHUNK
    return q_off[:, None] - k_off[None, :], k_off


def band_attention(q, k, v, n_prev, bias, sinks):
    b, s, hq, d = q.shape
    hkv = k.shape[2]
    grp = hq // hkv
    nc = s // CHUNK
    band = (n_prev + 1) * CHUNK
    pad = n_prev * CHUNK
    kp = jnp.pad(k, ((0, 0), (pad, 0), (0, 0), (0, 0)))
    vp = jnp.pad(v, ((0, 0), (pad, 0), (0, 0), (0, 0)))
    qc = jnp.moveaxis(q.reshape(b, nc, CHUNK, hkv, grp, d), 1, 0)
    _, k_off = band_offsets(n_prev)
    scale = d ** -0.5

    def one_chunk(args):
        c, qb = args
        start = c * CHUNK
        kb = lax.dynamic_slice_in_dim(kp, start, band, axis=1)
        vb = lax.dynamic_slice_in_dim(vp, start, band, axis=1)
        sc = jnp.einsum("bqkgd,bskd->bkgqs", qb, kb).astype(jnp.float32) * scale + bias
        valid = (start + k_off) >= 0
        sc = jnp.where(valid, sc, -jnp.inf)
        if sinks is None:
            p = jax.nn.softmax(sc, axis=-1)
        else:
            snk = sinks.astype(jnp.float32)[None, :, :, None, None]
            m = jnp.maximum(jnp.max(sc, axis=-1, keepdims=True), snk)
            e = jnp.exp(sc - m)
            p = e / (jnp.sum(e, axis=-1, keepdims=True) + jnp.exp(snk - m))
        return jnp.einsum("bkgqs,bskd->bqkgd", p.astype(vb.dtype), vb)

    out = lax.map(one_chunk, (jnp.arange(nc, dtype=jnp.int32), qc))
    return jnp.moveaxis(out, 0, 1).reshape(b, s, hq * d)


def attn_layer(h, w_in, w_out, relpos_table, q_norm_a, k_norm_a, q_norm_b, k_norm_b, sinks):
    b, s, _ = h.shape
    da, db, dkv = H_A * HEAD_DIM, H_B * HEAD_DIM, H_B_KV * HEAD_DIM
    cuts = [da, 2 * da, 3 * da, 3 * da + db, 3 * da + db + dkv]
    qa, ka, va, qb, kb, vb = jnp.split(h @ w_in, cuts, axis=-1)
    heads = lambda t, n: t.reshape(b, s, n, HEAD_DIM)
    qa = rmsnorm(heads(qa, H_A), q_norm_a)
    ka = rmsnorm(heads(ka, H_A), k_norm_a)
    rel_a, _ = band_offsets(A_PREV_CHUNKS)
    idx = jnp.clip(rel_a, -MAX_REL_DIST, MAX_REL_DIST) + MAX_REL_DIST
    bias_a = relpos_table.astype(jnp.float32)[:, idx][:, None]
    oa = band_attention(qa, ka, heads(va, H_A), A_PREV_CHUNKS, bias_a, None)
    qb = rmsnorm(heads(qb, H_B), q_norm_b)
    kb = rmsnorm(heads(kb, H_B_KV), k_norm_b)
    rel_b, _ = band_offsets(B_PREV_CHUNKS)
    slopes = 2.0 ** (-8.0 * jnp.arange(1, H_B + 1, dtype=jnp.float32) / H_B)
    bias_b = (-slopes[:, None, None] * jnp.abs(rel_b).astype(jnp.float32)).reshape(
        H_B_KV, H_B // H_B_KV, CHUNK, (B_PREV_CHUNKS + 1) * CHUNK)
    ob = band_attention(qb, kb, heads(vb, H_B_KV), B_PREV_CHUNKS, bias_b,
                        sinks.reshape(H_B_KV, H_B // H_B_KV))
    return jnp.concatenate([oa, ob], axis=-1) @ w_out


def ssd_scan(x, dt, a, bm, cm):
    b, s, h, p = x.shape
    g, n = bm.shape[2], bm.shape[3]
    r = h // g
    L = SSD_CHUNK
    nc = s // L

    def to_chunks(t):
        return jnp.moveaxis(t.reshape((b, nc, L) + t.shape[2:]), 1, 0)

    xc = to_chunks(x.reshape(b, s, g, r, p))
    dtc = to_chunks(dt.reshape(b, s, g, r))
    bc, cc = to_chunks(bm), to_chunks(cm)
    a = a.reshape(g, r)
    causal = jnp.tril(jnp.ones((L, L), dtype=bool))[None, :, :, None, None]

    def step(state, inp):
        xk, dtk, bk, ck = inp
        acs = jnp.cumsum(dtk * a, axis=1)
        seg = acs[:, :, None] - acs[:, None, :]
        decay = jnp.exp(jnp.where(causal, seg, -jnp.inf))
        cb = jnp.einsum("blgn,bsgn->bgls", ck, bk)
        y_intra = jnp.einsum("bgls,blsgr,bsgrp->blgrp", cb, decay, xk * dtk[..., None])
        y_state = jnp.einsum("blgn,bgrpn->blgrp", ck, state) * jnp.exp(acs)[..., None]
        last = acs[:, -1]
        w_in = jnp.exp(last[:, None] - acs) * dtk
        new_state = state * jnp.exp(last)[..., None, None] + jnp.einsum(
            "bsgn,bsgr,bsgrp->bgrpn", bk, w_in, xk)
        return new_state, y_intra + y_state

    state0 = jnp.zeros((b, g, r, p, n), jnp.float32)
    _, ys = lax.scan(step, state0, (xc, dtc, bc, cc))
    return jnp.moveaxis(ys, 0, 1).reshape(b, s, h, p)


def ssm_layer(h, w_in, conv_w, conv_b, dt_bias, a_log, d_skip, norm_w, w_out):
    b, s, _ = h.shape
    z, xbc, dt = jnp.split(h @ w_in, [D_INNER, D_INNER + SSM_CONV_CH], axis=-1)
    xbc = jax.nn.silu(causal_dwconv(xbc, conv_w, conv_b))
    xs, bm, cm = jnp.split(xbc, [D_INNER, D_INNER + SSM_GROUPS * SSM_STATE], axis=-1)
    xs = xs.reshape(b, s, SSM_HEADS, SSM_HEAD_DIM).astype(jnp.float32)
    bm = bm.reshape(b, s, SSM_GROUPS, SSM_STATE).astype(jnp.float32)
    cm = cm.reshape(b, s, SSM_GROUPS, SSM_STATE).astype(jnp.float32)
    dt = jax.nn.softplus(dt.astype(jnp.float32) + dt_bias.astype(jnp.float32))
    a = -jnp.exp(a_log.astype(jnp.float32))
    y = ssd_scan(xs, dt, a, bm, cm) + d_skip.astype(jnp.float32)[:, None] * xs
    y = y.reshape(b, s, D_INNER) * jax.nn.silu(z.astype(jnp.float32))
    yg = y.reshape(b, s, SSM_GROUPS, D_INNER // SSM_GROUPS)
    yg = yg * lax.rsqrt(jnp.mean(yg * yg, axis=-1, keepdims=True) + EPS)
    y = (yg.reshape(b, s, D_INNER) * norm_w.astype(jnp.float32)).astype(h.dtype)
    return y @ w_out


def conv_ffn(h, w_in, conv_w, conv_b, w_out):
    gate, up = jnp.split(h @ w_in, [D_FF], axis=-1)
    gate = causal_dwconv(gate, conv_w, conv_b)
    return (jax.nn.silu(gate) * up) @ w_out


def setup_inputs(seed: int = 0) -> dict:
    key = jax.random.key(seed)
    ks = jax.random.split(key, 24)
    n_even, n_odd = (DEPTH + 1) // 2, DEPTH // 2
    nrm = lambda k, shape, scale: jax.random.normal(k, shape, jnp.float32) * scale
    dt0 = jnp.exp(jax.random.uniform(ks[15], (n_odd, SSM_HEADS), jnp.float32,
                                     np.log(1e-3), np.log(1e-1)))
    return {
        "x": nrm(ks[0], (BATCH, SEQ, D_MODEL), 1.0),
        "norm_mix": 1.0 + nrm(ks[1], (DEPTH, D_MODEL), 0.05),
        "norm_ffn": 1.0 + nrm(ks[2], (DEPTH, D_MODEL), 0.05),
        "attn_w_in": nrm(ks[3], (n_even, D_MODEL, ATTN_PROJ), D_MODEL ** -0.5),
        "attn_w_out": nrm(ks[4], (n_even, (H_A + H_B) * HEAD_DIM, D_MODEL), ((H_A + H_B) * HEAD_DIM) ** -0.5),
        "relpos_table": nrm(ks[5], (n_even, H_A, 2 * MAX_REL_DIST + 1), 0.2),
        "q_norm_a": 1.0 + nrm(ks[6], (n_even, HEAD_DIM), 0.05),
        "k_norm_a": 1.0 + nrm(ks[7], (n_even, HEAD_DIM), 0.05),
        "q_norm_b": 1.0 + nrm(ks[8], (n_even, HEAD_DIM), 0.05),
        "k_norm_b": 1.0 + nrm(ks[9], (n_even, HEAD_DIM), 0.05),
        "sinks": nrm(ks[10], (n_even, H_B), 0.5),
        "ssm_w_in": nrm(ks[11], (n_odd, D_MODEL, SSM_PROJ), D_MODEL ** -0.5),
        "ssm_conv_w": nrm(ks[12], (n_odd, SSM_CONV, SSM_CONV_CH), SSM_CONV ** -0.5),
        "ssm_conv_b": nrm(ks[13], (n_odd, SSM_CONV_CH), 0.02),
        "ssm_dt_bias": dt0 + jnp.log(-jnp.expm1(-dt0)),
        "ssm_a_log": jnp.log(jax.random.uniform(ks[14], (n_odd, SSM_HEADS), jnp.float32, 1.0, 16.0)),
        "ssm_d": 1.0 + nrm(ks[16], (n_odd, SSM_HEADS), 0.1),
        "ssm_norm": 1.0 + nrm(ks[17], (n_odd, D_INNER), 0.05),
        "ssm_w_out": nrm(ks[18], (n_odd, D_INNER, D_MODEL), D_INNER ** -0.5),
        "ffn_w_in": nrm(ks[19], (DEPTH, D_MODEL, 2 * D_FF), D_MODEL ** -0.5),
        "ffn_conv_w": nrm(ks[20], (DEPTH, FFN_CONV, D_FF), FFN_CONV ** -0.5),
        "ffn_conv_b": nrm(ks[21], (DEPTH, D_FF), 0.02),
        "ffn_w_out": nrm(ks[22], (DEPTH, D_FF, D_MODEL), D_FF ** -0.5),
    }


def reference(x, norm_mix, norm_ffn, attn_w_in, attn_w_out, relpos_table, q_norm_a, k_norm_a,
              q_norm_b, k_norm_b, sinks, ssm_w_in, ssm_conv_w, ssm_conv_b, ssm_dt_bias, ssm_a_log,
              ssm_d, ssm_norm, ssm_w_out, ffn_w_in, ffn_conv_w, ffn_conv_b, ffn_w_out):
    for layer in range(DEPTH):
        i = layer // 2
        h = rmsnorm(x, norm_mix[layer])
        if layer % 2 == 0:
            mix = attn_layer(h, attn_w_in[i], attn_w_out[i], relpos_table[i], q_norm_a[i],
                             k_norm_a[i], q_norm_b[i], k_norm_b[i], sinks[i])
        else:
            mix = ssm_layer(h, ssm_w_in[i], ssm_conv_w[i], ssm_conv_b[i], ssm_dt_bias[i],
                            ssm_a_log[i], ssm_d[i], ssm_norm[i], ssm_w_out[i])
        x = x + mix
        h = rmsnorm(x, norm_ffn[layer])
        x = x + conv_ffn(h, ffn_w_in[layer], ffn_conv_w[layer], ffn_conv_b[layer], ffn_w_out[layer])
    return x
```

```python
from contextlib import ExitStack
import numpy as np
import concourse.bass as bass
import concourse.mybir as mybir
from concourse.bass_utils import run_bass_kernel_spmd

F32 = mybir.dt.float32
BF16 = mybir.dt.bfloat16
ALU = mybir.AluOpType
AF = mybir.ActivationFunctionType

D = 1024
SEQ = 8192
NCORES = 8
TT = 512
EPS = 1e-6
DFF = 2816
NFC = DFF // 128
NEG = -30000.0

SEM_WRAP = 30000
ENGS = ("pe", "act", "dve", "pool", "sp")


class Buf:
    __slots__ = ("name", "w", "rs", "dsem", "dcnt", "excl")

    def __init__(self, name, excl=False):
        self.name = name
        self.w = None
        self.rs = []
        self.dsem = None
        self.dcnt = 0
        self.excl = excl


class Op:
    __slots__ = ("eng", "fn", "waits", "sigkey", "sigord", "signaled", "dma", "dbuf", "sigval")

    def __init__(self, eng, fn):
        self.eng = eng
        self.fn = fn
        self.waits = []
        self.signaled = False
        self.dma = False
        self.dbuf = None
        self.sigval = None


class Prog:
    def __init__(self, nc):
        self.nc = nc
        self.ops = {e: [] for e in ENGS}
        self.wm = {e: {} for e in ENGS}
        self.dma_bufs = []

    def _dep(self, op, prod):
        if prod is None or prod is op:
            return
        if (not prod.dma) and prod.eng == op.eng and op.eng == "pe":
            return
        wm = self.wm[op.eng]
        if wm.get(prod.sigkey, -1) >= prod.sigord:
            return
        wm[prod.sigkey] = prod.sigord
        prod.signaled = True
        op.waits.append(prod)

    def _track(self, o, reads, writes, after):
        for p in after:
            self._dep(o, p)
        for b in reads:
            self._dep(o, b.w)
            if b.excl:
                for r in b.rs:
                    if r.eng != o.eng:
                        self._dep(o, r)
        for b in writes:
            self._dep(o, b.w)
            for r in b.rs:
                self._dep(o, r)
        for b in reads:
            b.rs.append(o)
        for b in writes:
            b.w = o
            b.rs = []

    def op(self, eng, fn, reads=(), writes=(), after=()):
        o = Op(eng, fn)
        lst = self.ops[eng]
        o.sigkey = eng
        o.sigord = len(lst)
        self._track(o, reads, writes, after)
        lst.append(o)
        return o

    def dma(self, eng, fn, dbuf, reads=(), writes=(), after=()):
        o = Op(eng, fn)
        o.dma = True
        o.dbuf = dbuf
        if dbuf.dsem is None:
            dbuf.dsem = True
            self.dma_bufs.append(dbuf)
        dbuf.dcnt += 1
        o.sigkey = ("d", id(dbuf))
        o.sigord = dbuf.dcnt
        o.sigval = 16 * dbuf.dcnt
        self._track(o, reads, writes, after)
        self.ops[eng].append(o)
        return o

    def emit(self, final_waits=()):
        nc = self.nc
        self.op("sp", None, after=final_waits)
        with ExitStack() as st:
            esems = {}
            for e in ENGS:
                n = 0
                for o in self.ops[e]:
                    if o.dma or not o.signaled:
                        continue
                    n += 1
                    o.sigval = n
                nsem = (n + SEM_WRAP - 1) // SEM_WRAP
                esems[e] = [st.enter_context(nc.semaphore(f"s_{e}_{i}")) for i in range(nsem)]
            for i, b in enumerate(self.dma_bufs):
                b.dsem = st.enter_context(nc.semaphore(f"d_{i}"))

            def sem_of(p):
                if p.dma:
                    return p.dbuf.dsem, p.sigval
                n = p.sigval
                return esems[p.eng][(n - 1) // SEM_WRAP], ((n - 1) % SEM_WRAP) + 1

            block = st.enter_context(nc.Block())
            ops = self.ops

            def run(engname):
                def body(eng):
                    for o in ops[engname]:
                        for p in o.waits:
                            s, v = sem_of(p)
                            eng.wait_ge(s, v)
                        if o.fn is None:
                            continue
                        ins = o.fn(eng)
                        if o.dma:
                            ins.then_inc(o.dbuf.dsem, 16)
                        elif o.signaled:
                            s, _ = sem_of(o)
                            ins.then_inc(s, 1)
                return body

            block.tensor(run("pe"))
            block.scalar(run("act"))
            block.vector(run("dve"))
            block.gpsimd(run("pool"))
            block.sync(run("sp"))


class Ring:
    def __init__(self, items):
        self.items = items
        self.i = 0

    def get(self):
        it = self.items[self.i % len(self.items)]
        self.i += 1
        return it


class Builder:
    def __init__(self, S, nstage):
        self.S = S
        self.nstage = nstage
        self.nc = bass.Bass("TRN2", target_bir_lowering=False)
        self.P = Prog(self.nc)
        self.st = ExitStack()
        self.consts = []
        self.wlist = []

    def sb(self, name, shape, dt):
        return self.st.enter_context(self.nc.sbuf_tensor(name, shape, dt))

    def ring(self, name, n, shape, dt):
        return Ring([(self.sb(f"{name}{i}", shape, dt), Buf(f"{name}{i}")) for i in range(n)])

    def din(self, name, shape, dt=F32):
        return self.nc.dram_tensor(name, list(shape), dt, kind="ExternalInput").ap()

    def palloc(self):
        return self.pfree_list.pop(0)

    def pfree(self, b):
        self.pfree_list.append(b)

    def mm(self, out, lhsT, rhs, start, stop, reads, writes):
        self.P.op("pe", lambda e: e.matmul(out, lhsT, rhs, start=start, stop=stop, skip_group_check=True),
                  reads=reads, writes=writes)

    def act(self, out, in_, func, reads, writes, **kw):
        self.P.op("act", lambda e: e.activation(out, in_, func, **kw), reads=reads, writes=writes)

    def tt(self, eng, out, in0, in1, op, reads, writes):
        self.P.op(eng, lambda e: e.tensor_tensor(out, in0, in1, op), reads=reads, writes=writes)

    def ts(self, eng, out, in0, s1, s2, op0, op1, reads, writes):
        if s2 is None:
            self.P.op(eng, lambda e: e.tensor_scalar(out, in0, s1, None, op0), reads=reads, writes=writes)
        else:
            self.P.op(eng, lambda e: e.tensor_scalar(out, in0, s1, s2, op0, op1), reads=reads, writes=writes)

    def stt(self, eng, out, in0, scalar, in1, op0, op1, reads, writes):
        self.P.op(eng, lambda e: e.scalar_tensor_tensor(out, in0, scalar, in1, op0, op1), reads=reads, writes=writes)

    def cp(self, eng, out, in_, reads, writes):
        if eng == "act":
            self.P.op("act", lambda e: e.copy(out, in_), reads=reads, writes=writes)
        else:
            self.P.op(eng, lambda e: e.tensor_copy(out, in_), reads=reads, writes=writes)

    def cload(self, name, shape, dt=F32, via_pool=False, src_shape=None):
        d = self.din(name, shape if src_shape is None else src_shape, F32)
        t = self.sb("c_" + name, shape, dt)
        b = Buf(name)
        eng = "pool" if (via_pool or dt != F32) else "sp"
        self.P.dma(eng, lambda e: e.dma_start(out=t[:], in_=d), b, writes=[b])
        self.consts.append(b)
        return t, b

    def weight(self, name, K, F):
        nc = self.nc
        src = self.din(name, [K, F], F32)
        dst = nc.dram_tensor(name + "_bf", [K, F], BF16, kind="Internal").ap()
        w = {"ap": dst, "src": src, "snap": None, "K": K, "F": F, "ready": False}
        self.wlist.append(w)
        return w

    def emit_casts(self):
        for w in self.wlist:
            K, F, dst, src = w["K"], w["F"], w["ap"], w["src"]
            rows = max(128, (1 << 19) // F // 128 * 128)
            if w is self.wlist[0]:
                rows = 128
            for r0 in range(0, K, rows):
                r1 = min(K, r0 + rows)
                lane = self.lanes.get()[1]
                self.P.dma("pool", lambda e, r0=r0, r1=r1, dst=dst, src=src: e.dma_start(out=dst[r0:r1, :], in_=src[r0:r1, :]),
                           lane, writes=[lane])
            w["snap"] = [l[1].w for l in self.lanes.items if l[1].w is not None]

    def wready(self, w):
        if not w["ready"]:
            self.P.op("sp", None, after=w["snap"])
            w["ready"] = True

    def panel_in(self, w, c0, ncols):
        self.wready(w)
        t, b = self.wslots.get()
        v = t[:, 0:8 * ncols].rearrange("p (k n) -> p k n", k=8)
        src = w["ap"].rearrange("(k p) f -> p k f", p=128)[:, :, c0:c0 + ncols]
        self.P.dma("sp", lambda e: e.dma_start(out=v, in_=src), b, writes=[b])
        return v, b

    def panel_out(self, w, kc0, nk, c0):
        self.wready(w)
        t, b = self.wslots.get()
        v = t[:, 0:nk * 512].rearrange("p (k n) -> p k n", k=nk)
        src = w["ap"].rearrange("(k p) f -> p k f", p=128)[:, kc0:kc0 + nk, c0:c0 + 512]
        self.P.dma("sp", lambda e: e.dma_start(out=v, in_=src), b, writes=[b])
        return v, b

    def build(self):
        nc, P, S = self.nc, self.P, self.S
        NT = S // TT
        self.lanes = Ring([(None, Buf(f"lane{i}")) for i in range(8)])
        xT_d = self.din("xT", [D, S])
        out_d = nc.dram_tensor("outT", [D, S], F32, kind="ExternalOutput").ap()

        pall = self.st.enter_context(nc.psum_tensor("pall", [128, 8, 512], F32))
        self.pbanks = [(pall[:, i, :], Buf(f"ps{i}", excl=True)) for i in range(8)]
        self.pfree_list = list(self.pbanks)

        wA_F = self.weight("wA_F", D, 1792)
        wA_V = self.weight("wA_V", D, 640)
        wA_O = self.weight("wA_O", D, D)
        wF_in = [None, None]
        wF_out = [None, None]
        wF_in[0] = self.weight("wF_in0", D, 2 * DFF)
        wF_out[0] = self.weight("wF_out0", DFF, D)
        if self.nstage >= 3:
            wS_z = self.weight("wS_z", D, 2048)
            wS_x = self.weight("wS_x", D, 3072)
            wS_dt = self.weight("wS_dt", D, 32)
            wS_O = self.weight("wS_O", 2048, D)
            wF_in[1] = self.weight("wF_in1", D, 2 * DFF)
            wF_out[1] = self.weight("wF_out1", DFF, D)

        gains, gains_b = self.cload("gains", [128, 32])
        qkg, qkg_b = self.cload("qkg", [128, 4])
        esink, esink_b = self.cload("sinkc", [128, 4])
        BTA, BTA_b = self.cload("BTA", [128, 8, 640], BF16)
        BTB, BTB_b = self.cload("BTB", [128, 8, 256], BF16)
        fcw, fcw_b = self.cload("fcw", [128, 2, NFC, 3])
        fcb, fcb_b = self.cload("fcb", [128, 2, NFC])
        ident = self.sb("ident", [128, 128], BF16)
        ident_b = Buf("ident")
        onesD = self.sb("onesD", [128, 128], BF16)
        onesD_b = Buf("onesD")
        blk64 = self.sb("blk64", [128, 128], BF16)
        blk64_b = Buf("blk64")
        ones_bf = self.sb("ones_bf", [128, 64], BF16)
        ones_bf_b = Buf("ones_bf")
        P.op("pool", lambda e: e.memset(ident[:], 1.0), writes=[ident_b])
        P.op("pool", lambda e: e.affine_select(ident[:], ident[:], [[1, 128]], ALU.is_equal, 0.0, base=0,
                                               channel_multiplier=-1), reads=[ident_b], writes=[ident_b])
        P.op("pool", lambda e: e.memset(onesD[:], 1.0 / D), writes=[onesD_b])
        P.op("pool", lambda e: e.memset(blk64[:], 0.0), writes=[blk64_b])
        P.op("pool", lambda e: e.memset(blk64[0:64, 0:64], 1.0 / 64), reads=[blk64_b], writes=[blk64_b])
        P.op("pool", lambda e: e.memset(blk64[64:128, 64:128], 1.0 / 64), reads=[blk64_b], writes=[blk64_b])
        P.op("pool", lambda e: e.memset(ones_bf[:], 1.0), writes=[ones_bf_b])
        P.op("dve", lambda e: e.tensor_scalar(qkg[:, 0:1], qkg[:, 0:1], 0.125, None, ALU.mult), reads=[qkg_b], writes=[qkg_b])
        P.op("dve", lambda e: e.tensor_scalar(qkg[:, 2:3], qkg[:, 2:3], 0.125, None, ALU.mult), reads=[qkg_b], writes=[qkg_b])
        P.op("act", lambda e: e.activation(esink[:], esink[:], AF.Exp), reads=[esink_b], writes=[esink_b])
        self.consts += [ident_b, onesD_b, blk64_b, ones_bf_b]
        P.op("act", lambda e: e.activation(BTA[:], BTA[:], AF.Exp), reads=[BTA_b], writes=[BTA_b])
        P.op("act", lambda e: e.activation(BTB[:], BTB[:], AF.Exp), reads=[BTB_b], writes=[BTB_b])

        xT = self.sb("xTs", [128, 8, TT], F32)
        xT_b = [Buf(f"xT{c}") for c in range(8)]
        hT = self.sb("hTs", [128, 8, TT], BF16)
        hT_b = [Buf(f"hT{c}") for c in range(8)]
        xtok_slots = [(hT[:, 4 * i:4 * i + 4, :].rearrange("p c t -> p (c t)"), hT_b[4 * i:4 * i + 4]) for i in range(2)]
        big = self.sb("big", [128, 24, TT], BF16)
        big_b = [Buf(f"big{c}") for c in range(24)]
        qT = self.sb("qTs", [128, 8, TT], BF16)
        qT_b = [Buf(f"qT{c}") for c in range(8)]
        kA = self.sb("kA", [128, 4, 2, TT], BF16)
        kA_b = [[Buf(f"kA{c}_{s}") for s in range(2)] for c in range(4)]
        kB = self.sb("kB", [128, 2, 2, TT], BF16)
        kB_b = [[Buf(f"kB{c}_{s}") for s in range(2)] for c in range(2)]
        vA = self.sb("vA", [128, 2, 4, 512], BF16)
        vA_b = [[Buf(f"vA{s}_{t}") for t in range(4)] for s in range(2)]
        vB = self.sb("vB", [128, 2, 4, 128], BF16)
        vB_b = [[Buf(f"vB{s}_{t}") for t in range(4)] for s in range(2)]
        ftail = self.sb("ftail", [128, 2, NFC, 2], F32)
        ftail_b = [[Buf(f"ftail{l}_{j}") for j in range(NFC)] for l in range(2)]
        P.op("pool", lambda e: e.memset(ftail[:], 0.0), writes=[b for l in ftail_b for b in l])

        self.wslots = self.ring("wslot", 4, [128, 4096], BF16)
        f32r = self.ring("f32r", 6, [128, 516], F32)
        bf16r = self.ring("bf16r", 5, [128, 512], BF16)
        if self.nstage >= 3:
            n_c1 = len(self.consts)
            scw, scw_b = self.cload("scw", [128, 24, 4])
            scb, scb_b = self.cload("scb", [128, 24])
            hpar, hpar_b = self.cload("hpar", [128, 3, 32])
            normw, normw_b = self.cload("normw", [128, 2048], BF16)
            P.op("act", lambda e: e.activation(hpar[:, 1, :], hpar[:, 1, :], AF.Exp), reads=[hpar_b], writes=[hpar_b])
            P.op("dve", lambda e: e.tensor_scalar(hpar[:, 1, :], hpar[:, 1, :], -1.0, None, ALU.mult), reads=[hpar_b], writes=[hpar_b])
            tri = self.sb("tri", [128, 128], F32)
            tri_b = Buf("tri")
            Um = self.sb("Um", [128, 128], F32)
            Um_b = Buf("Um")
            onesf = self.sb("onesf", [128, 128], F32)
            onesf_b = Buf("onesf")
            P.op("pool", lambda e: e.memset(tri[:], 1.0), writes=[tri_b])
            P.op("pool", lambda e: e.affine_select(tri[:], tri[:], [[1, 128]], ALU.is_ge, 0.0, base=0,
                                                   channel_multiplier=-1), reads=[tri_b], writes=[tri_b])
            P.op("pool", lambda e: e.memset(Um[:], 1.0), writes=[Um_b])
            P.op("pool", lambda e: e.affine_select(Um[:], Um[:], [[-1, 128]], ALU.is_gt, 0.0, base=0,
                                                   channel_multiplier=1), reads=[Um_b], writes=[Um_b])
            P.op("pool", lambda e: e.memset(onesf[:], 1.0), writes=[onesf_b])
            dcol, dcol_b = self.cload("dcol", [128, 16])
            diagD = self.sb("diagD", [128, 16, 128], BF16)
            diagD_b = Buf("diagD")
            for c_ in range(16):
                P.op("dve", lambda e, c_=c_: e.tensor_scalar(diagD[:, c_, :], ident[:], dcol[:, c_:c_ + 1], None, ALU.mult),
                     reads=[ident_b, dcol_b], writes=[diagD_b])
            self.consts += [diagD_b]
            self.consts += [tri_b, Um_b, onesf_b, hpar_b]
            self.consts2 = self.consts[n_c1:]
            self.consts = self.consts[:n_c1]
            Sst = self.sb("Sst", [128, 2048], F32)
            Sst_b = [Buf(f"S{g}") for g in range(4)]
            Sbf = self.sb("Sbf", [128, 2048], BF16)
            Sbf_b = [Buf(f"Sbf{g}") for g in range(4)]
            P.op("pool", lambda e: e.memset(Sst[:], 0.0), writes=Sst_b)
            P.op("pool", lambda e: e.memset(Sbf[:], 0.0), writes=Sbf_b)
            stail = self.sb("stail", [128, 24, 3], F32)
            stail_b = [Buf(f"stail{c}") for c in range(24)]
            P.op("pool", lambda e: e.memset(stail[:], 0.0), writes=stail_b)
            zs = self.sb("zs", [128, 4, 2048], BF16)
            zs_b = [Buf(f"zs{t}") for t in range(4)]
            btok_r = self.ring("btok", 2, [128, 512], BF16)
            yn_r = self.ring("yn", 1, [128, 2048], BF16)
            mt_r = self.ring("mt", 8, [128, 512], BF16)
            xg_r = self.ring("xg", 4, [128, 512], BF16)
            R_r = Ring([(qT[:, 2 * i:2 * i + 2, :].rearrange("p c t -> p (c t)").bitcast(F32), [qT_b[2 * i], qT_b[2 * i + 1]])
                        for i in range(4)])
            dt4, dA4, acs4, E4, el4, ww4, lndt4 = [self.sb(n, [128, 128], F32) for n in ("dt4", "dA4", "acs4", "E4", "el4", "ww4", "lndt4")]
            dt4_b, dA4_b, acs4_b, E4_b, el4_b, ww4_b, lndt4_b = [Buf(n) for n in ("dt4", "dA4", "acs4", "E4", "el4", "ww4", "lndt4")]
            lnhi = self.sb("lnhi", [128, 128], BF16)
            lnlo = self.sb("lnlo", [128, 128], BF16)
            lnhl_b = Buf("lnhl")
            junk = self.sb("junk", [128, 512], BF16)
            junk_b = Buf("junk")
            cbm_r = self.ring("cbm", 2, [128, 512], BF16)
            ss_r = self.ring("ss", 3, [128, 8], F32)

        xsrc0 = xT_d.rearrange("(c p) t -> p c t", p=128)[:, :, 0:TT]
        P.dma("sp", lambda e: e.dma_start(out=xT[:], in_=xsrc0), xT_b[0], writes=xT_b)
        self.emit_casts()
        for e in ENGS:
            P.op(e, None, reads=self.consts + [gains_b, qkg_b, esink_b])

        fins = []
        for T in range(NT):
            t0 = T * TT
            cur, prev = T % 2, 1 - (T % 2)
            CHUNK_IO = self.nstage >= 4
            if T > 0 and not CHUNK_IO:
                xsrc = xT_d.rearrange("(c p) t -> p c t", p=128)[:, :, t0:t0 + TT]
                P.dma("sp", lambda e, xsrc=xsrc: e.dma_start(out=xT[:], in_=xsrc), xT_b[0], writes=xT_b)

            def xio(oc, T=T, t0=t0):
                dst = out_d[oc * 128:(oc + 1) * 128, t0:t0 + TT]
                fins.append(P.dma("act", lambda e: e.dma_start(out=dst, in_=xT[:, oc, :]), xT_b[oc], reads=[xT_b[oc]]))
                if T + 1 < NT:
                    src = xT_d[oc * 128:(oc + 1) * 128, t0 + TT:t0 + 2 * TT]
                    P.dma("act", lambda e: e.dma_start(out=xT[:, oc, :], in_=src), xT_b[oc], writes=[xT_b[oc]])

            def rmsnorm(gcol0):
                for c in range(8):
                    self.act(hT[:, c, :], xT[:, c, :], AF.Square, [xT_b[c]], [hT_b[c]])
                ms, ms_b = self.palloc()
                for c in range(8):
                    self.mm(ms, onesD[:], hT[:, c, :], c == 0, c == 7, [hT_b[c]], [ms_b])
                (r1, r1_b), (r2, r2_b) = f32r.get(), f32r.get()
                self.act(r1[:, 0:TT], ms, AF.Ln, [ms_b], [r1_b], bias=EPS)
                self.pfree((ms, ms_b))
                self.act(r2[:, 0:TT], r1[:, 0:TT], AF.Exp, [r1_b], [r2_b], scale=-0.5)
                for c in range(8):
                    self.stt("dve", hT[:, c, :], xT[:, c, :], gains[:, gcol0 + c:gcol0 + c + 1], r2[:, 0:TT],
                             ALU.mult, ALU.mult, [xT_b[c], r2_b], [hT_b[c]])

            def out_proj(w, nkc, src_b, after_add=None):
                for oh in range(2):
                    pss = [self.palloc() for _ in range(4)]
                    for k0 in range(0, nkc, 8):
                        nk = min(8, nkc - k0)
                        pv, pb = self.panel_out(w, k0, nk, oh * 512)
                        for i in range(nk):
                            kc = k0 + i
                            for o in range(4):
                                self.mm(pss[o][0], pv[:, i, o * 128:(o + 1) * 128], big[:, kc, :], kc == 0,
                                        kc == nkc - 1, [pb, src_b[kc]], [pss[o][1]])
                    for o in range(4):
                        oc = oh * 4 + o
                        self.tt("dve", xT[:, oc, :], xT[:, oc, :], pss[o][0], ALU.add, [pss[o][1], xT_b[oc]], [xT_b[oc]])
                        self.pfree(pss[o])
                        if after_add is not None:
                            after_add(oc)

            rmsnorm(0)
            dests = ([(qT[:, c, :], qT_b[c], 0) for c in range(4)] +
                     [(kA[:, c, cur, :], kA_b[c][cur], 1) for c in range(4)] +
                     [(qT[:, 4 + c, :], qT_b[4 + c], 2) for c in range(4)] +
                     [(kB[:, c, cur, :], kB_b[c][cur], 3) for c in range(2)])
            pend_qk = []

            def qk_tail(ps, ps_b, sq, sq_b, dst, dst_b, gi):
                ms, ms_b = self.palloc()
                self.mm(ms, blk64[:], sq[:], True, True, [sq_b], [ms_b])
                (r1, r1_b), (r2, r2_b) = f32r.get(), f32r.get()
                self.act(r1[:, 0:TT], ms, AF.Ln, [ms_b], [r1_b], bias=EPS)
                self.pfree((ms, ms_b))
                self.act(r2[:, 0:TT], r1[:, 0:TT], AF.Exp, [r1_b], [r2_b], scale=-0.5)
                self.stt("dve", dst, ps, qkg[:, gi:gi + 1], r2[:, 0:TT], ALU.mult, ALU.mult,
                         [ps_b, r2_b], [dst_b])
                self.pfree((ps, ps_b))

            for pi in range(4):
                ncols = 512 if pi < 3 else 256
                pv, pb = self.panel_in(wA_F, pi * 512, ncols)
                for j in range(ncols // 128):
                    dst, dst_b, gi = dests[pi * 4 + j]
                    ps, ps_b = self.palloc()
                    for kc in range(8):
                        self.mm(ps, pv[:, kc, j * 128:(j + 1) * 128], hT[:, kc, :], kc == 0, kc == 7,
                                [pb, hT_b[kc]], [ps_b])
                    sq, sq_b = bf16r.get()
                    self.act(sq[:], ps, AF.Square, [ps_b], [sq_b])
                    pend_qk.append((ps, ps_b, sq, sq_b, dst, dst_b, gi))
                    if len(pend_qk) > 1:
                        qk_tail(*pend_qk.pop(0))
            while pend_qk:
                qk_tail(*pend_qk.pop(0))
            pv, pb = self.panel_in(wA_V, 0, 512)
            pv2, pb2 = self.panel_in(wA_V, 512, 128)
            for tb in range(4):
                ps, ps_b = self.palloc()
                for kc in range(8):
                    self.mm(ps, hT[:, kc, tb * 128:(tb + 1) * 128], pv[:, kc, :], kc == 0, kc == 7,
                            [pb, hT_b[kc]], [ps_b])
                self.cp("act", vA[:, cur, tb, :], ps, [ps_b], [vA_b[cur][tb]])
                self.pfree((ps, ps_b))
                ps, ps_b = self.palloc()
                for kc in range(8):
                    self.mm(ps[:, 0:128], hT[:, kc, tb * 128:(tb + 1) * 128], pv2[:, kc, :], kc == 0, kc == 7,
                            [pb2, hT_b[kc]], [ps_b])
                self.cp("dve", vB[:, cur, tb, :], ps[:, 0:128], [ps_b], [vB_b[cur][tb]])
                self.pfree((ps, ps_b))

            def attn_pair(cidx, heads):
                po, po_b = self.palloc()
                pd, pd_b = self.palloc()

                def fr(hd, ki):
                    rows = hd["rows"]
                    k_ap, k_b, v_ap, v_b, q0, N, b_ap = hd["kt"][ki]
                    ps, ps_b = self.palloc()
                    self.mm(ps[:, 0:N], k_ap, hd["q"][rows:rows + 64, q0:q0 + N], True, True,
                            [k_b, hd["q_b"]], [ps_b])
                    pT, pT_b = bf16r.get()
                    self.act(pT[:, 0:N], ps[:, 0:N], AF.Exp, [ps_b], [pT_b])
                    self.pfree((ps, ps_b))
                    self.tt("dve", pT[:, 0:N], pT[:, 0:N], b_ap, ALU.mult, [pT_b], [pT_b])
                    return pT, pT_b

                def bk(hd, ki, pT, pT_b):
                    rows = hd["rows"]
                    nk = len(hd["kt"])
                    k_ap, k_b, v_ap, v_b, q0, N, b_ap = hd["kt"][ki]
                    self.mm(po[rows:rows + 64, q0:q0 + N], v_ap, pT[:, 0:N], ki == 0, ki == nk - 1,
                            [v_b, pT_b], [po_b])
                    self.mm(pd[rows:rows + 64, q0:q0 + N], ones_bf[:], pT[:, 0:N], ki == 0, ki == nk - 1,
                            [pT_b], [pd_b])

                pend = []
                for hd in heads:
                    for ki in range(len(hd["kt"])):
                        pend.append((hd, ki) + fr(hd, ki))
                        if len(pend) > 3:
                            bk(*pend.pop(0))
                while pend:
                    bk(*pend.pop(0))
                rec, rec_b = f32r.get()
                if cidx >= 4:
                    self.act(rec[:, 0:TT], pd, AF.Ln, [pd_b], [rec_b], bias=esink[:, cidx - 4:cidx - 3])
                else:
                    self.act(rec[:, 0:TT], pd, AF.Ln, [pd_b], [rec_b])
                self.act(rec[:, 0:TT], rec[:, 0:TT], AF.Exp, [rec_b], [rec_b], scale=-1.0)
                self.pfree((pd, pd_b))
                self.tt("dve", big[:, cidx, :], po, rec[:, 0:TT], ALU.mult, [po_b, rec_b], [big_b[cidx]])
                self.pfree((po, po_b))

            for c in range(4):
                heads = []
                for half in range(2):
                    h = 2 * c + half
                    rows = 64 * half
                    kt = []
                    for j in range(8):
                        if T == 0 and j < 4:
                            continue
                        slot = prev if j < 4 else cur
                        jj = j % 4
                        i0, i1 = max(0, 2 * j - 8), min(7, 2 * j + 1)
                        N = 64 * (i1 - i0 + 1)
                        d0 = 8 + i0 - 2 * j
                        kt.append((kA[rows:rows + 64, c, slot, jj * 128:(jj + 1) * 128], kA_b[c][slot],
                                   vA[:, slot, jj, h * 64:(h + 1) * 64], vA_b[slot][jj],
                                   64 * i0, N, BTA[:, h, 64 * d0:64 * d0 + N]))
                    heads.append({"rows": rows, "kt": kt, "q": qT[:, c, :], "q_b": qT_b[c]})
                attn_pair(c, heads)
            for c in range(4):
                heads = []
                kv = c // 2
                for half in range(2):
                    hq = 2 * c + half
                    rows = 64 * half
                    kt = []
                    for j in range(-1, 4):
                        if T == 0 and j < 0:
                            continue
                        slot = prev if j < 0 else cur
                        jj = 3 if j < 0 else j
                        i0, i1 = max(0, 2 * j), min(7, 2 * j + 3)
                        N = 64 * (i1 - i0 + 1)
                        d0 = i0 - 2 * j
                        kt.append((kB[rows:rows + 64, kv, slot, jj * 128:(jj + 1) * 128], kB_b[kv][slot],
                                   vB[:, slot, jj, kv * 64:(kv + 1) * 64], vB_b[slot][jj],
                                   64 * i0, N, BTB[:, hq, 64 * d0:64 * d0 + N]))
                    heads.append({"rows": rows, "kt": kt, "q": qT[:, 4 + c, :], "q_b": qT_b[4 + c]})
                attn_pair(4 + c, heads)
            out_proj(wA_O, 8, big_b)

            def ffn(l):
                rmsnorm(16 + 8 * l)
                for pi in range(NFC // 2):
                    pv, pb = self.panel_in(wF_in[l], pi * 512, 512)
                    pss = []
                    for j in range(4):
                        ps, ps_b = self.palloc()
                        for kc in range(8):
                            self.mm(ps, pv[:, kc, j * 128:(j + 1) * 128], hT[:, kc, :], kc == 0, kc == 7,
                                    [pb, hT_b[kc]], [ps_b])
                        pss.append((ps, ps_b))
                    for jj in range(2):
                        fc = 2 * pi + jj
                        pg, pg_b = pss[jj]
                        pu, pu_b = pss[2 + jj]
                        gb, gb_b = f32r.get()
                        tl_b = ftail_b[l][fc]
                        self.cp("pool", gb[:, 0:2], ftail[:, l, fc, :], [tl_b], [gb_b])
                        self.cp("act", gb[:, 2:2 + TT], pg, [pg_b], [gb_b])
                        self.pfree((pg, pg_b))
                        self.cp("pool", ftail[:, l, fc, :], gb[:, TT:TT + 2], [gb_b], [tl_b])
                        t1, t1_b = f32r.get()
                        self.act(t1[:, 0:TT], gb[:, 0:TT], AF.Identity, [gb_b], [t1_b],
                                 scale=fcw[:, l, fc, 0:1], bias=fcb[:, l, fc:fc + 1])
                        self.stt("dve", t1[:, 0:TT], gb[:, 1:1 + TT], fcw[:, l, fc, 1:2], t1[:, 0:TT],
                                 ALU.mult, ALU.add, [gb_b, t1_b], [t1_b])
                        self.stt("dve", t1[:, 0:TT], gb[:, 2:2 + TT], fcw[:, l, fc, 2:3], t1[:, 0:TT],
                                 ALU.mult, ALU.add, [gb_b, t1_b], [t1_b])
                        self.act(t1[:, 0:TT], t1[:, 0:TT], AF.Silu, [t1_b], [t1_b])
                        self.tt("dve", big[:, fc, :], t1[:, 0:TT], pu, ALU.mult, [t1_b, pu_b], [big_b[fc]])
                        self.pfree((pu, pu_b))
                out_proj(wF_out[l], NFC, big_b, after_add=(xio if (l == 1 and CHUNK_IO) else None))

            if self.nstage >= 2:
                ffn(0)

            def ssd():
                rmsnorm(8)
                pvd, pbd = self.panel_in(wS_dt, 0, 32)
                identb = ident
                ps, ps_b = self.palloc()
                for tb in range(4):
                    for kc in range(8):
                        self.mm(ps[:, tb * 32:(tb + 1) * 32], hT[:, kc, tb * 128:(tb + 1) * 128], pvd[:, kc, :],
                                kc == 0, kc == 7, [pbd, hT_b[kc]], [ps_b])
                bc4 = lambda ap: ap.unsqueeze(1).to_broadcast([128, 4, 32])
                v4 = lambda ap: ap.rearrange("p (t h) -> p t h", t=4)
                self.tt("dve", v4(dt4[:]), v4(ps[:, 0:128]), bc4(hpar[:, 0, :]), ALU.add, [ps_b], [dt4_b])
                self.pfree((ps, ps_b))
                self.act(dt4[:], dt4[:], AF.Exp, [dt4_b], [dt4_b])
                self.act(dt4[:], dt4[:], AF.Ln, [dt4_b], [dt4_b], bias=1.0)
                self.tt("dve", v4(dA4[:]), v4(dt4[:]), bc4(hpar[:, 1, :]), ALU.mult, [dt4_b], [dA4_b])
                self.act(lndt4[:], dt4[:], AF.Ln, [dt4_b], [lndt4_b])
                self.cp("dve", lnhi[:], lndt4[:], [lndt4_b], [lnhl_b])
                self.tt("dve", lnlo[:], lndt4[:], lnhi[:], ALU.subtract, [lndt4_b, lnhl_b], [lnhl_b])
                ps, ps_b = self.palloc()
                self.mm(ps[:, 0:128], tri[:], dA4[:], True, True, [dA4_b], [ps_b])
                self.mm(ps[:, 128:256], onesf[:], dA4[:], True, True, [dA4_b], [ps_b])
                self.cp("act", acs4[:], ps[:, 0:128], [ps_b], [acs4_b])
                self.act(E4[:], ps[:, 0:128], AF.Exp, [ps_b], [E4_b])
                self.act(el4[:], ps[:, 128:256], AF.Exp, [ps_b], [el4_b])
                self.cp("act", ww4[:], ps[:, 128:256], [ps_b], [ww4_b])
                self.pfree((ps, ps_b))
                self.tt("dve", ww4[:], ww4[:], acs4[:], ALU.subtract, [ww4_b, acs4_b], [ww4_b])
                self.act(ww4[:], ww4[:], AF.Exp, [ww4_b], [ww4_b])
                self.tt("dve", ww4[:], ww4[:], dt4[:], ALU.mult, [ww4_b, dt4_b], [ww4_b])
                pend_conv = []
                for pi in range(6):
                    pv, pb = self.panel_in(wS_x, pi * 512, 512)
                    for j in range(4):
                        ch = pi * 4 + j
                        ps, ps_b = self.palloc()
                        for kc in range(8):
                            self.mm(ps, pv[:, kc, j * 128:(j + 1) * 128], hT[:, kc, :], kc == 0, kc == 7,
                                    [pb, hT_b[kc]], [ps_b])
                        gb, gb_b = f32r.get()
                        self.cp("pool", gb[:, 0:3], stail[:, ch, :], [stail_b[ch]], [gb_b])
                        self.cp("act", gb[:, 3:3 + TT], ps, [ps_b], [gb_b])
                        self.pfree((ps, ps_b))
                        self.cp("pool", stail[:, ch, :], gb[:, TT:TT + 3], [gb_b], [stail_b[ch]])
                        t1, t1_b = f32r.get()
                        self.act(t1[:, 0:TT], gb[:, 0:TT], AF.Identity, [gb_b], [t1_b],
                                 scale=scw[:, ch, 0:1], bias=scb[:, ch:ch + 1])
                        for k in range(1, 4):
                            self.stt("dve", t1[:, 0:TT], gb[:, k:k + TT], scw[:, ch, k:k + 1], t1[:, 0:TT],
                                     ALU.mult, ALU.add, [gb_b, t1_b], [t1_b])
                        if pend_conv:
                            pt1, pt1_b, pch = pend_conv.pop(0)
                            self.act(big[:, pch, :], pt1[:, 0:TT], AF.Silu, [pt1_b], [big_b[pch]])
                        pend_conv.append((t1, t1_b, ch))
                while pend_conv:
                    pt1, pt1_b, pch = pend_conv.pop(0)
                    self.act(big[:, pch, :], pt1[:, 0:TT], AF.Silu, [pt1_b], [big_b[pch]])
                for pz in range(4):
                    pv, pb = self.panel_in(wS_z, pz * 512, 512)
                    for tb in range(4):
                        ps, ps_b = self.palloc()
                        for kc in range(8):
                            self.mm(ps, hT[:, kc, tb * 128:(tb + 1) * 128], pv[:, kc, :], kc == 0, kc == 7,
                                    [pb, hT_b[kc]], [ps_b])
                        self.act(zs[:, tb, pz * 512:(pz + 1) * 512], ps, AF.Silu, [ps_b], [zs_b[tb]])
                        self.pfree((ps, ps_b))
                blk = {}

                def front0(tb):
                    tc0 = tb * 128
                    st_ = {}
                    xtok, xtok_b = xtok_slots[tb % 2]
                    for hb in range(2):
                        ps, ps_b = self.palloc()
                        psb = ps.bitcast(BF16)
                        for i in range(8):
                            c = hb * 8 + i
                            P.op("pe", lambda e, psb=psb, i=i, c=c, tc0=tc0: e.transpose(psb[:, i * 128:(i + 1) * 128], big[:, c, tc0:tc0 + 128], identb[:]),
                                 reads=[big_b[c]], writes=[ps_b])
                        self.cp("act", xtok[:, hb * 1024:(hb + 1) * 1024], psb[:, 0:1024], [ps_b], xtok_b)
                        self.pfree((ps, ps_b))
                    btok, btok_b = btok_r.get()
                    ps, ps_b = self.palloc()
                    psb = ps.bitcast(BF16)
                    for g in range(4):
                        P.op("pe", lambda e, psb=psb, g=g, tc0=tc0: e.transpose(psb[:, g * 128:(g + 1) * 128], big[:, 16 + g, tc0:tc0 + 128], identb[:]),
                             reads=[big_b[16 + g]], writes=[ps_b])
                    self.cp("act", btok[:], psb[:, 0:512], [ps_b], [btok_b])
                    self.pfree((ps, ps_b))
                    ps, ps_b = self.palloc()
                    for g in range(4):
                        self.mm(ps[:, g * 128:(g + 1) * 128], big[:, 16 + g, tc0:tc0 + 128], big[:, 20 + g, tc0:tc0 + 128],
                                True, True, [big_b[16 + g], big_b[20 + g]], [ps_b])
                    cbm, cbm_b = cbm_r.get()
                    self.tt("dve", cbm[:].rearrange("p (g l) -> p g l", g=4), ps.rearrange("p (g l) -> p g l", g=4),
                            tri[:].unsqueeze(1).to_broadcast([128, 4, 128]), ALU.mult, [ps_b], [cbm_b])
                    self.pfree((ps, ps_b))
                    ss, ss_b = ss_r.get()
                    P.op("pool", lambda e, ss=ss: e.memset(ss[:], 0.0), writes=[ss_b])
                    st_.update(xtok=xtok, xtok_b=xtok_b, btok=btok, btok_b=btok_b, cbm=cbm, cbm_b=cbm_b, ss=ss, ss_b=ss_b, ys=[], mt={})
                    blk[tb] = st_

                def quadR(tb, q):
                    h0 = tb * 32 + q * 4
                    Rq, Rq_b = R_r.get()
                    P.op("dve", lambda e, Rq=Rq, h0=h0: e.tensor_tensor(
                        Rq.rearrange("p (r l) -> p r l", r=4), tri[:].unsqueeze(1).to_broadcast([128, 4, 128]),
                        dA4[:, h0:h0 + 4].unsqueeze(2).to_broadcast([128, 4, 128]), ALU.mult),
                        reads=[dA4_b], writes=Rq_b)
                    ps, ps_b = self.palloc()
                    self.mm(ps, Um[:], Rq, True, False, Rq_b, [ps_b])
                    self.mm(ps.rearrange("p (r l) -> p r l", r=4), ident[:],
                            lnhi[:, h0:h0 + 4].unsqueeze(2).to_broadcast([128, 4, 128]), False, False, [lnhl_b], [ps_b])
                    self.mm(ps.rearrange("p (r l) -> p r l", r=4), ident[:],
                            lnlo[:, h0:h0 + 4].unsqueeze(2).to_broadcast([128, 4, 128]), False, True, [lnhl_b], [ps_b])
                    dec, dec_b = bf16r.get()
                    self.act(dec[:], ps, AF.Exp, [ps_b], [dec_b])
                    self.pfree((ps, ps_b))
                    return dec, dec_b

                def quadM(tb, q, dec, dec_b):
                    st_ = blk[tb]
                    cbm, cbm_b = st_["cbm"], st_["cbm_b"]
                    g = q // 2
                    mt, mt_b = mt_r.get()
                    self.tt("dve", mt[:].rearrange("p (r l) -> p r l", r=4), dec[:].rearrange("p (r l) -> p r l", r=4),
                            cbm[:, g * 128:(g + 1) * 128].unsqueeze(1).to_broadcast([128, 4, 128]), ALU.mult,
                            [dec_b, cbm_b], [mt_b])
                    st_["mt"][q] = (mt, mt_b)

                def xprep(tb, g):
                    st_ = blk[tb]
                    xtok, xtok_b = st_["xtok"], st_["xtok_b"]
                    gc = slice(g * 512, (g + 1) * 512)
                    hs = slice(tb * 32 + g * 8, tb * 32 + (g + 1) * 8)
                    xg3 = xtok[:, gc].rearrange("p (r d) -> p r d", r=8)
                    xw, xw_b = xg_r.get()
                    self.tt("pool", xw[:].rearrange("p (r d) -> p r d", r=8), xg3,
                            ww4[:, hs].unsqueeze(2).to_broadcast([128, 8, 64]), ALU.mult, xtok_b + [ww4_b], [xw_b])
                    st_[g] = (xw, xw_b)
                    self.tt("pool", Sst[:, gc].rearrange("p (r d) -> p r d", r=8), Sst[:, gc].rearrange("p (r d) -> p r d", r=8),
                            el4[:, hs].unsqueeze(2).to_broadcast([128, 8, 64]), ALU.mult, [Sst_b[g], el4_b], [Sst_b[g]])

                def back(tb, g):
                    st_ = blk[tb]
                    tc0 = tb * 128
                    xw, xw_b = st_[g]
                    xtok, xtok_b = st_["xtok"], st_["xtok_b"]
                    mts = [st_["mt"][2 * g], st_["mt"][2 * g + 1]]
                    btok, btok_b, ss, ss_b = st_["btok"], st_["btok_b"], st_["ss"], st_["ss_b"]
                    gc = slice(g * 512, (g + 1) * 512)
                    hs = slice(tb * 32 + g * 8, tb * 32 + (g + 1) * 8)
                    py, py_b = self.palloc()
                    for r in range(8):
                        mt, mt_b = mts[r // 4]
                        self.mm(py[:, r * 64:(r + 1) * 64], mt[:, (r % 4) * 128:(r % 4 + 1) * 128],
                                xtok[:, g * 512 + r * 64:g * 512 + (r + 1) * 64], r == 0, False, [mt_b] + xtok_b, [py_b])
                    for cc in range(4):
                        c = 4 * g + cc
                        self.mm(py[:, cc * 128:(cc + 1) * 128], big[:, c, tc0:tc0 + 128], diagD[:, c, :], False, cc == 3,
                                [big_b[c]], [py_b])
                    pg, pg_b = self.palloc()
                    self.mm(pg, big[:, 20 + g, tc0:tc0 + 128], Sbf[:, gc], True, True, [big_b[20 + g], Sbf_b[g]], [pg_b])
                    pd_, pd_b = self.palloc()
                    self.mm(pd_, btok[:, g * 128:(g + 1) * 128], xw[:], True, True, [btok_b, xw_b], [pd_b])
                    yt, yt_b = f32r.get()
                    self.tt("dve", yt[:, 0:512].rearrange("p (r d) -> p r d", r=8), pg.rearrange("p (r d) -> p r d", r=8),
                            E4[:, hs].unsqueeze(2).to_broadcast([128, 8, 64]), ALU.mult, [pg_b, E4_b], [yt_b])
                    self.pfree((pg, pg_b))
                    self.tt("dve", yt[:, 0:512], yt[:, 0:512], py, ALU.add, [py_b, yt_b], [yt_b])
                    self.pfree((py, py_b))
                    self.tt("pool", yt[:, 0:512], yt[:, 0:512], zs[:, tb, gc], ALU.mult, [yt_b, zs_b[tb]], [yt_b])
                    P.op("act", lambda e, yt=yt, ss=ss, g=g: e.activation(junk[:], yt[:, 0:512], AF.Square, accum_out=ss[:, g:g + 1]),
                         reads=[yt_b], writes=[junk_b, ss_b])
                    st_["ys"].append((yt, yt_b))
                    self.tt("dve", Sst[:, gc], Sst[:, gc], pd_, ALU.add, [Sst_b[g], pd_b], [Sst_b[g]])
                    self.pfree((pd_, pd_b))
                    self.cp("act", Sbf[:, gc], Sst[:, gc], [Sst_b[g]], [Sbf_b[g]])

                def epi(tb):
                    st_ = blk[tb]
                    tc0 = tb * 128
                    ss, ss_b = st_["ss"], st_["ss_b"]
                    yn, yn_b = yn_r.get()
                    self.act(ss[:, 0:4], ss[:, 0:4], AF.Ln, [ss_b], [ss_b], scale=1.0 / 512, bias=EPS)
                    self.act(ss[:, 0:4], ss[:, 0:4], AF.Exp, [ss_b], [ss_b], scale=-0.5)
                    for g in range(4):
                        yt, yt_b = st_["ys"][g]
                        gc = slice(g * 512, (g + 1) * 512)
                        self.stt("dve", yn[:, gc], yt[:, 0:512], ss[:, g:g + 1], normw[:, gc], ALU.mult, ALU.mult,
                                 [yt_b, ss_b], [yn_b])
                    for hb in range(2):
                        ps, ps_b = self.palloc()
                        psb = ps.bitcast(BF16)
                        for i in range(8):
                            c = hb * 8 + i
                            P.op("pe", lambda e, psb=psb, i=i, c=c, yn=yn: e.transpose(psb[:, i * 128:(i + 1) * 128], yn[:, c * 128:(c + 1) * 128], identb[:]),
                                 reads=[yn_b], writes=[ps_b])
                        P.op("act" if hb == 0 else "dve", (lambda e, psb=psb, hb=hb, tc0=tc0: (e.copy if hb == 0 else e.tensor_copy)(
                            big[:, hb * 8:(hb + 1) * 8, tc0:tc0 + 128], psb[:, 0:1024].rearrange("p (c t) -> p c t", c=8))),
                            reads=[ps_b], writes=big_b[hb * 8:(hb + 1) * 8])
                        self.pfree((ps, ps_b))

                LAG = 3
                front0(0)
                for tb in range(4):
                    for g in range(4):
                        xprep(tb, g)
                    if tb < 3:
                        front0(tb + 1)
                    pq = []
                    for q in range(8):
                        pq.append((tb, q) + quadR(tb, q))
                        if len(pq) > LAG:
                            quadM(*pq.pop(0))
                    while pq:
                        quadM(*pq.pop(0))
                    if tb > 0:
                        epi(tb - 1)
                    for g in range(4):
                        back(tb, g)
                epi(3)
                out_proj(wS_O, 16, big_b)

            if self.nstage >= 3:
                if T == 0:
                    for e in ENGS:
                        P.op(e, None, reads=self.consts2)
                ssd()
            if self.nstage >= 4:
                ffn(1)
            if not CHUNK_IO:
                dsto = out_d.rearrange("(c p) t -> p c t", p=128)[:, :, t0:t0 + TT]
                fins.append(P.dma("sp", lambda e, dsto=dsto: e.dma_start(out=dsto, in_=xT[:]), xT_b[0], reads=xT_b))
        P.emit(fins)
        self.st.close()
        return nc


def _prep_shared(inp, nstage):
    f = lambda a: np.ascontiguousarray(np.asarray(a, dtype=np.float32))
    m = {}
    aw = f(inp["attn_w_in"][0])
    qa, ka, va, qb, kb, vb = np.split(aw, [512, 1024, 1536, 2048, 2176], axis=1)
    m["wA_F"] = f(np.concatenate([qa, ka, qb, kb[:, 0:64], kb[:, 0:64], kb[:, 64:128], kb[:, 64:128]], axis=1))
    m["wA_V"] = f(np.concatenate([va, vb], axis=1))
    m["wA_O"] = f(inp["attn_w_out"][0])
    for l in range(2):
        w = f(inp["ffn_w_in"][l])
        g, u = w[:, :DFF], w[:, DFF:]
        cols = []
        for pi in range(NFC // 2):
            cols += [g[:, pi * 256:(pi + 1) * 256], u[:, pi * 256:(pi + 1) * 256]]
        m[f"wF_in{l}"] = f(np.concatenate(cols, axis=1))
        m[f"wF_out{l}"] = f(inp["ffn_w_out"][l])
    gains = np.zeros((128, 32), np.float32)
    for l in range(2):
        gains[:, l * 8:(l + 1) * 8] = f(inp["norm_mix"][l]).reshape(8, 128).T
        gains[:, 16 + l * 8:16 + (l + 1) * 8] = f(inp["norm_ffn"][l]).reshape(8, 128).T
    m["gains"] = gains
    m["qkg"] = f(np.stack([np.tile(f(inp[k][0]), 2) for k in ("q_norm_a", "k_norm_a", "q_norm_b", "k_norm_b")], axis=1))
    m["sinkc"] = f(np.repeat(f(inp["sinks"][0]), 64).reshape(4, 128).T)
    tbl = f(inp["relpos_table"][0])
    kl = np.arange(128)[:, None]
    col = np.arange(640)[None, :]
    rel = col - kl
    idx = np.clip(rel, -256, 256) + 256
    bta = tbl[:, idx]
    dl = (col // 64) * np.ones_like(kl)
    mask = ((kl < 64) & (dl == 9)) | ((kl >= 64) & (dl == 0))
    bta = np.where(mask[None], np.float32(NEG), bta)
    m["BTA"] = f(bta.transpose(1, 0, 2))
    col = np.arange(256)[None, :]
    rel = col - kl
    slopes = (2.0 ** (-8.0 * np.arange(1, 9, dtype=np.float32) / 8)).astype(np.float32)
    btb = -slopes[:, None, None] * np.abs(rel).astype(np.float32)[None]
    dl = (col // 64) * np.ones_like(kl)
    mask = ((kl < 64) & (dl == 3)) | ((kl >= 64) & (dl == 0))
    btb = np.where(mask[None], np.float32(NEG), btb)
    m["BTB"] = f(btb.transpose(1, 0, 2))
    if nstage >= 3:
        sw = f(inp["ssm_w_in"][0])
        m["wS_z"] = f(sw[:, 0:2048])
        m["wS_x"] = f(sw[:, 2048:5120])
        m["wS_dt"] = f(sw[:, 5120:5152])
        m["wS_O"] = f(inp["ssm_w_out"][0])
        m["scw"] = f(f(inp["ssm_conv_w"][0]).reshape(4, 24, 128).transpose(2, 1, 0))
        m["scb"] = f(f(inp["ssm_conv_b"][0]).reshape(24, 128).T)
        hp = np.stack([f(inp["ssm_dt_bias"][0]), f(inp["ssm_a_log"][0]), f(inp["ssm_d"][0])], axis=0)
        m["hpar"] = f(np.broadcast_to(hp[None], (128, 3, 32)))
        m["dcol"] = f(np.repeat(f(inp["ssm_d"][0]), 64).reshape(16, 128).T)
        m["normw"] = f(np.broadcast_to(f(inp["ssm_norm"][0])[None], (128, 2048)))
    m["fcw"] = f(np.stack([f(inp["ffn_conv_w"][l]).reshape(3, NFC, 128).transpose(2, 1, 0) for l in range(2)], axis=1))
    m["fcb"] = f(np.stack([f(inp["ffn_conv_b"][l]).reshape(NFC, 128).T for l in range(2)], axis=1))
    return m


_CACHE = {}


def run(inputs, S=SEQ, nstage=4, ncores=NCORES):
    key = (S, nstage)
    if key not in _CACHE:
        _CACHE[key] = Builder(S, nstage).build()
    nc = _CACHE[key]
    shared = _prep_shared(inputs, nstage)
    x = np.asarray(inputs["x"], dtype=np.float32)
    in_maps = []
    for b in range(ncores):
        m = dict(shared)
        m["xT"] = np.ascontiguousarray(x[b, :S, :].T)
        in_maps.append(m)
    res = run_bass_kernel_spmd(nc, in_maps, core_ids=list(range(ncores)))
    out = np.stack([np.asarray(r["outT"], dtype=np.float32).T for r in res.results], axis=0)
    return out


def kernel(**inputs):
    return run(inputs)
```

```python
from contextlib import ExitStack
import numpy as np
import concourse.bass as bass
import concourse.mybir as mybir
from concourse.bass_utils import run_bass_kernel_spmd

F32 = mybir.dt.float32
BF16 = mybir.dt.bfloat16
ALU = mybir.AluOpType
AF = mybir.ActivationFunctionType

D = 1024
SEQ = 8192
NCORES = 8
TT = 512
EPS = 1e-6
DFF = 2816
NFC = DFF // 128
NEG = -30000.0

SEM_WRAP = 30000
ENGS = ("pe", "act", "dve", "pool", "sp")


class Buf:
    __slots__ = ("name", "w", "rs", "dsem", "dcnt", "excl")

    def __init__(self, name, excl=False):
        self.name = name
        self.w = None
        self.rs = []
        self.dsem = None
        self.dcnt = 0
        self.excl = excl


class Op:
    __slots__ = ("eng", "fn", "waits", "sigkey", "sigord", "signaled", "dma", "dbuf", "sigval")

    def __init__(self, eng, fn):
        self.eng = eng
        self.fn = fn
        self.waits = []
        self.signaled = False
        self.dma = False
        self.dbuf = None
        self.sigval = None


class Prog:
    def __init__(self, nc):
        self.nc = nc
        self.ops = {e: [] for e in ENGS}
        self.wm = {e: {} for e in ENGS}
        self.dma_bufs = []

    def _dep(self, op, prod):
        if prod is None or prod is op:
            return
        if (not prod.dma) and prod.eng == op.eng and op.eng == "pe":
            return
        wm = self.wm[op.eng]
        if wm.get(prod.sigkey, -1) >= prod.sigord:
            return
        wm[prod.sigkey] = prod.sigord
        prod.signaled = True
        op.waits.append(prod)

    def _track(self, o, reads, writes, after):
        for p in after:
            self._dep(o, p)
        for b in reads:
            self._dep(o, b.w)
            if b.excl:
                for r in b.rs:
                    if r.eng != o.eng:
                        self._dep(o, r)
        for b in writes:
            self._dep(o, b.w)
            for r in b.rs:
                self._dep(o, r)
        for b in reads:
            b.rs.append(o)
        for b in writes:
            b.w = o
            b.rs = []

    def op(self, eng, fn, reads=(), writes=(), after=()):
        o = Op(eng, fn)
        lst = self.ops[eng]
        o.sigkey = eng
        o.sigord = len(lst)
        self._track(o, reads, writes, after)
        lst.append(o)
        return o

    def dma(self, eng, fn, dbuf, reads=(), writes=(), after=()):
        o = Op(eng, fn)
        o.dma = True
        o.dbuf = dbuf
        if dbuf.dsem is None:
            dbuf.dsem = True
            self.dma_bufs.append(dbuf)
        dbuf.dcnt += 1
        o.sigkey = ("d", id(dbuf))
        o.sigord = dbuf.dcnt
        o.sigval = 16 * dbuf.dcnt
        self._track(o, reads, writes, after)
        self.ops[eng].append(o)
        return o

    def emit(self, final_waits=()):
        nc = self.nc
        self.op("sp", None, after=final_waits)
        with ExitStack() as st:
            esems = {}
            for e in ENGS:
                n = 0
                for o in self.ops[e]:
                    if o.dma or not o.signaled:
                        continue
                    n += 1
                    o.sigval = n
                nsem = (n + SEM_WRAP - 1) // SEM_WRAP
                esems[e] = [st.enter_context(nc.semaphore(f"s_{e}_{i}")) for i in range(nsem)]
            for i, b in enumerate(self.dma_bufs):
                b.dsem = st.enter_context(nc.semaphore(f"d_{i}"))

            def sem_of(p):
                if p.dma:
                    return p.dbuf.dsem, p.sigval
                n = p.sigval
                return esems[p.eng][(n - 1) // SEM_WRAP], ((n - 1) % SEM_WRAP) + 1

            block = st.enter_context(nc.Block())
            ops = self.ops

            def run(engname):
                def body(eng):
                    for o in ops[engname]:
                        for p in o.waits:
                            s, v = sem_of(p)
                            eng.wait_ge(s, v)
                        if o.fn is None:
                            continue
                        ins = o.fn(eng)
                        if o.dma:
                            ins.then_inc(o.dbuf.dsem, 16)
                        elif o.signaled:
                            s, _ = sem_of(o)
                            ins.then_inc(s, 1)
                return body

            block.tensor(run("pe"))
            block.scalar(run("act"))
            block.vector(run("dve"))
            block.gpsimd(run("pool"))
            block.sync(run("sp"))


class Ring:
    def __init__(self, items):
        self.items = items
        self.i = 0

    def get(self):
        it = self.items[self.i % len(self.items)]
        self.i += 1
        return it


class Builder:
    def __init__(self, S, nstage):
        self.S = S
        self.nstage = nstage
        self.nc = bass.Bass("TRN2", target_bir_lowering=False)
        self.P = Prog(self.nc)
        self.st = ExitStack()
        self.consts = []
        self.wlist = []

    def sb(self, name, shape, dt):
        return self.st.enter_context(self.nc.sbuf_tensor(name, shape, dt))

    def ring(self, name, n, shape, dt):
        return Ring([(self.sb(f"{name}{i}", shape, dt), Buf(f"{name}{i}")) for i in range(n)])

    def din(self, name, shape, dt=F32):
        return self.nc.dram_tensor(name, list(shape), dt, kind="ExternalInput").ap()

    def palloc(self):
        return self.pfree_list.pop(0)

    def pfree(self, b):
        self.pfree_list.append(b)

    def mm(self, out, lhsT, rhs, start, stop, reads, writes):
        self.P.op("pe", lambda e: e.matmul(out, lhsT, rhs, start=start, stop=stop, skip_group_check=True),
                  reads=reads, writes=writes)

    def act(self, out, in_, func, reads, writes, **kw):
        self.P.op("act", lambda e: e.activation(out, in_, func, **kw), reads=reads, writes=writes)

    def tt(self, eng, out, in0, in1, op, reads, writes):
        self.P.op(eng, lambda e: e.tensor_tensor(out, in0, in1, op), reads=reads, writes=writes)

    def ts(self, eng, out, in0, s1, s2, op0, op1, reads, writes):
        if s2 is None:
            self.P.op(eng, lambda e: e.tensor_scalar(out, in0, s1, None, op0), reads=reads, writes=writes)
        else:
            self.P.op(eng, lambda e: e.tensor_scalar(out, in0, s1, s2, op0, op1), reads=reads, writes=writes)

    def stt(self, eng, out, in0, scalar, in1, op0, op1, reads, writes):
        self.P.op(eng, lambda e: e.scalar_tensor_tensor(out, in0, scalar, in1, op0, op1), reads=reads, writes=writes)

    def cp(self, eng, out, in_, reads, writes):
        if eng == "act":
            self.P.op("act", lambda e: e.copy(out, in_), reads=reads, writes=writes)
        else:
            self.P.op(eng, lambda e: e.tensor_copy(out, in_), reads=reads, writes=writes)

    def cload(self, name, shape, dt=F32, via_pool=False, src_shape=None):
        d = self.din(name, shape if src_shape is None else src_shape, F32)
        t = self.sb("c_" + name, shape, dt)
        b = Buf(name)
        eng = "pool" if (via_pool or dt != F32) else "sp"
        self.P.dma(eng, lambda e: e.dma_start(out=t[:], in_=d), b, writes=[b])
        self.consts.append(b)
        return t, b

    def weight(self, name, K, F):
        nc = self.nc
        src = self.din(name, [K, F], F32)
        dst = nc.dram_tensor(name + "_bf", [K, F], BF16, kind="Internal").ap()
        w = {"ap": dst, "src": src, "snap": None, "K": K, "F": F, "ready": False}
        self.wlist.append(w)
        return w

    def emit_casts(self):
        for w in self.wlist:
            K, F, dst, src = w["K"], w["F"], w["ap"], w["src"]
            rows = max(128, (1 << 19) // F // 128 * 128)
            for r0 in range(0, K, rows):
                r1 = min(K, r0 + rows)
                lane = self.lanes.get()[1]
                self.P.dma("pool", lambda e, r0=r0, r1=r1, dst=dst, src=src: e.dma_start(out=dst[r0:r1, :], in_=src[r0:r1, :]),
                           lane, writes=[lane])
            w["snap"] = [l[1].w for l in self.lanes.items if l[1].w is not None]

    def wready(self, w):
        if not w["ready"]:
            self.P.op("sp", None, after=w["snap"])
            w["ready"] = True

    def panel_in(self, w, c0, ncols):
        self.wready(w)
        t, b = self.wslots.get()
        v = t[:, 0:8 * ncols].rearrange("p (k n) -> p k n", k=8)
        src = w["ap"].rearrange("(k p) f -> p k f", p=128)[:, :, c0:c0 + ncols]
        self.P.dma("sp", lambda e: e.dma_start(out=v, in_=src), b, writes=[b])
        return v, b

    def panel_out(self, w, kc0, nk, c0):
        self.wready(w)
        t, b = self.wslots.get()
        v = t[:, 0:nk * 512].rearrange("p (k n) -> p k n", k=nk)
        src = w["ap"].rearrange("(k p) f -> p k f", p=128)[:, kc0:kc0 + nk, c0:c0 + 512]
        self.P.dma("sp", lambda e: e.dma_start(out=v, in_=src), b, writes=[b])
        return v, b

    def build(self):
        nc, P, S = self.nc, self.P, self.S
        NT = S // TT
        self.lanes = Ring([(None, Buf(f"lane{i}")) for i in range(8)])
        xT_d = self.din("xT", [D, S])
        out_d = nc.dram_tensor("outT", [D, S], F32, kind="ExternalOutput").ap()

        pall = self.st.enter_context(nc.psum_tensor("pall", [128, 8, 512], F32))
        self.pbanks = [(pall[:, i, :], Buf(f"ps{i}", excl=True)) for i in range(8)]
        self.pfree_list = list(self.pbanks)

        wA_F = self.weight("wA_F", D, 1792)
        wA_V = self.weight("wA_V", D, 640)
        wA_O = self.weight("wA_O", D, D)
        wF_in = [None, None]
        wF_out = [None, None]
        wF_in[0] = self.weight("wF_in0", D, 2 * DFF)
        wF_out[0] = self.weight("wF_out0", DFF, D)
        if self.nstage >= 3:
            wS_z = self.weight("wS_z", D, 2048)
            wS_x = self.weight("wS_x", D, 3072)
            wS_dt = self.weight("wS_dt", D, 32)
            wS_O = self.weight("wS_O", 2048, D)
            wF_in[1] = self.weight("wF_in1", D, 2 * DFF)
            wF_out[1] = self.weight("wF_out1", DFF, D)

        gains, gains_b = self.cload("gains", [128, 32])
        qkg, qkg_b = self.cload("qkg", [128, 4])
        esink, esink_b = self.cload("sinkc", [128, 4])
        BTA, BTA_b = self.cload("BTA", [128, 8, 640], BF16)
        BTB, BTB_b = self.cload("BTB", [128, 8, 256], BF16)
        fcw, fcw_b = self.cload("fcw", [128, 2, NFC, 3])
        fcb, fcb_b = self.cload("fcb", [128, 2, NFC])
        ident = self.sb("ident", [128, 128], BF16)
        ident_b = Buf("ident")
        onesD = self.sb("onesD", [128, 128], BF16)
        onesD_b = Buf("onesD")
        blk64 = self.sb("blk64", [128, 128], BF16)
        blk64_b = Buf("blk64")
        ones_bf = self.sb("ones_bf", [128, 64], BF16)
        ones_bf_b = Buf("ones_bf")
        P.op("pool", lambda e: e.memset(ident[:], 1.0), writes=[ident_b])
        P.op("pool", lambda e: e.affine_select(ident[:], ident[:], [[1, 128]], ALU.is_equal, 0.0, base=0,
                                               channel_multiplier=-1), reads=[ident_b], writes=[ident_b])
        P.op("pool", lambda e: e.memset(onesD[:], 1.0 / D), writes=[onesD_b])
        P.op("pool", lambda e: e.memset(blk64[:], 0.0), writes=[blk64_b])
        P.op("pool", lambda e: e.memset(blk64[0:64, 0:64], 1.0 / 64), reads=[blk64_b], writes=[blk64_b])
        P.op("pool", lambda e: e.memset(blk64[64:128, 64:128], 1.0 / 64), reads=[blk64_b], writes=[blk64_b])
        P.op("pool", lambda e: e.memset(ones_bf[:], 1.0), writes=[ones_bf_b])
        P.op("dve", lambda e: e.tensor_scalar(qkg[:, 0:1], qkg[:, 0:1], 0.125, None, ALU.mult), reads=[qkg_b], writes=[qkg_b])
        P.op("dve", lambda e: e.tensor_scalar(qkg[:, 2:3], qkg[:, 2:3], 0.125, None, ALU.mult), reads=[qkg_b], writes=[qkg_b])
        P.op("act", lambda e: e.activation(esink[:], esink[:], AF.Exp), reads=[esink_b], writes=[esink_b])
        self.consts += [ident_b, onesD_b, blk64_b, ones_bf_b]
        P.op("act", lambda e: e.activation(BTA[:], BTA[:], AF.Exp), reads=[BTA_b], writes=[BTA_b])
        P.op("act", lambda e: e.activation(BTB[:], BTB[:], AF.Exp), reads=[BTB_b], writes=[BTB_b])

        xT = self.sb("xTs", [128, 8, TT], F32)
        xT_b = [Buf(f"xT{c}") for c in range(8)]
        hT = self.sb("hTs", [128, 8, TT], BF16)
        hT_b = [Buf(f"hT{c}") for c in range(8)]
        xtok_slots = [(hT[:, 4 * i:4 * i + 4, :].rearrange("p c t -> p (c t)"), hT_b[4 * i:4 * i + 4]) for i in range(2)]
        big = self.sb("big", [128, 24, TT], BF16)
        big_b = [Buf(f"big{c}") for c in range(24)]
        qT = self.sb("qTs", [128, 8, TT], BF16)
        qT_b = [Buf(f"qT{c}") for c in range(8)]
        kA = self.sb("kA", [128, 4, 2, TT], BF16)
        kA_b = [[Buf(f"kA{c}_{s}") for s in range(2)] for c in range(4)]
        kB = self.sb("kB", [128, 2, 2, TT], BF16)
        kB_b = [[Buf(f"kB{c}_{s}") for s in range(2)] for c in range(2)]
        vA = self.sb("vA", [128, 2, 4, 512], BF16)
        vA_b = [[Buf(f"vA{s}_{t}") for t in range(4)] for s in range(2)]
        vB = self.sb("vB", [128, 2, 4, 128], BF16)
        vB_b = [[Buf(f"vB{s}_{t}") for t in range(4)] for s in range(2)]
        ftail = self.sb("ftail", [128, 2, NFC, 2], F32)
        ftail_b = [[Buf(f"ftail{l}_{j}") for j in range(NFC)] for l in range(2)]
        P.op("pool", lambda e: e.memset(ftail[:], 0.0), writes=[b for l in ftail_b for b in l])

        self.wslots = self.ring("wslot", 4, [128, 4096], BF16)
        f32r = self.ring("f32r", 6, [128, 516], F32)
        bf16r = self.ring("bf16r", 5, [128, 512], BF16)
        if self.nstage >= 3:
            n_c1 = len(self.consts)
            scw, scw_b = self.cload("scw", [128, 24, 4])
            scb, scb_b = self.cload("scb", [128, 24])
            hpar, hpar_b = self.cload("hpar", [128, 3, 32])
            normw, normw_b = self.cload("normw", [128, 2048], BF16)
            P.op("act", lambda e: e.activation(hpar[:, 1, :], hpar[:, 1, :], AF.Exp), reads=[hpar_b], writes=[hpar_b])
            P.op("dve", lambda e: e.tensor_scalar(hpar[:, 1, :], hpar[:, 1, :], -1.0, None, ALU.mult), reads=[hpar_b], writes=[hpar_b])
            tri = self.sb("tri", [128, 128], F32)
            tri_b = Buf("tri")
            Um = self.sb("Um", [128, 128], F32)
            Um_b = Buf("Um")
            onesf = self.sb("onesf", [128, 128], F32)
            onesf_b = Buf("onesf")
            P.op("pool", lambda e: e.memset(tri[:], 1.0), writes=[tri_b])
            P.op("pool", lambda e: e.affine_select(tri[:], tri[:], [[1, 128]], ALU.is_ge, 0.0, base=0,
                                                   channel_multiplier=-1), reads=[tri_b], writes=[tri_b])
            P.op("pool", lambda e: e.memset(Um[:], 1.0), writes=[Um_b])
            P.op("pool", lambda e: e.affine_select(Um[:], Um[:], [[-1, 128]], ALU.is_gt, 0.0, base=0,
                                                   channel_multiplier=1), reads=[Um_b], writes=[Um_b])
            P.op("pool", lambda e: e.memset(onesf[:], 1.0), writes=[onesf_b])
            dcol, dcol_b = self.cload("dcol", [128, 16])
            diagD = self.sb("diagD", [128, 16, 128], BF16)
            diagD_b = Buf("diagD")
            for c_ in range(16):
                P.op("dve", lambda e, c_=c_: e.tensor_scalar(diagD[:, c_, :], ident[:], dcol[:, c_:c_ + 1], None, ALU.mult),
                     reads=[ident_b, dcol_b], writes=[diagD_b])
            self.consts += [diagD_b]
            self.consts += [tri_b, Um_b, onesf_b, hpar_b]
            self.consts2 = self.consts[n_c1:]
            self.consts = self.consts[:n_c1]
            Sst = self.sb("Sst", [128, 2048], F32)
            Sst_b = [Buf(f"S{g}") for g in range(4)]
            Sbf = self.sb("Sbf", [128, 2048], BF16)
            Sbf_b = [Buf(f"Sbf{g}") for g in range(4)]
            P.op("pool", lambda e: e.memset(Sst[:], 0.0), writes=Sst_b)
            P.op("pool", lambda e: e.memset(Sbf[:], 0.0), writes=Sbf_b)
            stail = self.sb("stail", [128, 24, 3], F32)
            stail_b = [Buf(f"stail{c}") for c in range(24)]
            P.op("pool", lambda e: e.memset(stail[:], 0.0), writes=stail_b)
            zs = self.sb("zs", [128, 4, 2048], BF16)
            zs_b = [Buf(f"zs{t}") for t in range(4)]
            btok_r = self.ring("btok", 2, [128, 512], BF16)
            yn_r = self.ring("yn", 1, [128, 2048], BF16)
            mt_r = self.ring("mt", 8, [128, 512], BF16)
            xg_r = self.ring("xg", 4, [128, 512], BF16)
            R_r = Ring([(qT[:, 2 * i:2 * i + 2, :].rearrange("p c t -> p (c t)").bitcast(F32), [qT_b[2 * i], qT_b[2 * i + 1]])
                        for i in range(4)])
            dt4, dA4, acs4, E4, el4, ww4, lndt4 = [self.sb(n, [128, 128], F32) for n in ("dt4", "dA4", "acs4", "E4", "el4", "ww4", "lndt4")]
            dt4_b, dA4_b, acs4_b, E4_b, el4_b, ww4_b, lndt4_b = [Buf(n) for n in ("dt4", "dA4", "acs4", "E4", "el4", "ww4", "lndt4")]
            lnhi = self.sb("lnhi", [128, 128], BF16)
            lnlo = self.sb("lnlo", [128, 128], BF16)
            lnhl_b = Buf("lnhl")
            junk = self.sb("junk", [128, 512], BF16)
            junk_b = Buf("junk")
            cbm_r = self.ring("cbm", 2, [128, 512], BF16)
            ss_r = self.ring("ss", 3, [128, 8], F32)

        xsrc0 = xT_d.rearrange("(c p) t -> p c t", p=128)[:, :, 0:TT]
        P.dma("sp", lambda e: e.dma_start(out=xT[:], in_=xsrc0), xT_b[0], writes=xT_b)
        self.emit_casts()
        for e in ENGS:
            P.op(e, None, reads=self.consts + [gains_b, qkg_b, esink_b])

        fins = []
        for T in range(NT):
            t0 = T * TT
            cur, prev = T % 2, 1 - (T % 2)
            CHUNK_IO = self.nstage >= 4
            if T > 0 and not CHUNK_IO:
                xsrc = xT_d.rearrange("(c p) t -> p c t", p=128)[:, :, t0:t0 + TT]
                P.dma("sp", lambda e, xsrc=xsrc: e.dma_start(out=xT[:], in_=xsrc), xT_b[0], writes=xT_b)

            def xio(oc, T=T, t0=t0):
                dst = out_d[oc * 128:(oc + 1) * 128, t0:t0 + TT]
                fins.append(P.dma("act", lambda e: e.dma_start(out=dst, in_=xT[:, oc, :]), xT_b[oc], reads=[xT_b[oc]]))
                if T + 1 < NT:
                    for lc in ([oc - 1] if oc % 4 else []) + ([oc] if oc % 4 == 3 else []):
                        src = xT_d[lc * 128:(lc + 1) * 128, t0 + TT:t0 + 2 * TT]
                        P.dma("act", lambda e, lc=lc, src=src: e.dma_start(out=xT[:, lc, :], in_=src), xT_b[lc], writes=[xT_b[lc]])

            def rmsnorm(gcol0):
                for c in range(8):
                    self.act(hT[:, c, :], xT[:, c, :], AF.Square, [xT_b[c]], [hT_b[c]])
                ms, ms_b = self.palloc()
                for c in range(8):
                    self.mm(ms, onesD[:], hT[:, c, :], c == 0, c == 7, [hT_b[c]], [ms_b])
                (r1, r1_b), (r2, r2_b) = f32r.get(), f32r.get()
                self.act(r1[:, 0:TT], ms, AF.Ln, [ms_b], [r1_b], bias=EPS)
                self.pfree((ms, ms_b))
                self.act(r2[:, 0:TT], r1[:, 0:TT], AF.Exp, [r1_b], [r2_b], scale=-0.5)
                for c in range(8):
                    self.stt("dve", hT[:, c, :], xT[:, c, :], gains[:, gcol0 + c:gcol0 + c + 1], r2[:, 0:TT],
                             ALU.mult, ALU.mult, [xT_b[c], r2_b], [hT_b[c]])

            def out_proj(w, nkc, src_b, after_add=None):
                for oh in range(2):
                    pss = [self.palloc() for _ in range(4)]
                    for k0 in range(0, nkc, 8):
                        nk = min(8, nkc - k0)
                        pv, pb = self.panel_out(w, k0, nk, oh * 512)
                        for i in range(nk):
                            kc = k0 + i
                            for o in range(4):
                                self.mm(pss[o][0], pv[:, i, o * 128:(o + 1) * 128], big[:, kc, :], kc == 0,
                                        kc == nkc - 1, [pb, src_b[kc]], [pss[o][1]])
                    for o in range(4):
                        oc = oh * 4 + o
                        self.tt("dve", xT[:, oc, :], xT[:, oc, :], pss[o][0], ALU.add, [pss[o][1], xT_b[oc]], [xT_b[oc]])
                        self.pfree(pss[o])
                        if after_add is not None:
                            after_add(oc)

            rmsnorm(0)
            dests = ([(qT[:, c, :], qT_b[c], 0) for c in range(4)] +
                     [(kA[:, c, cur, :], kA_b[c][cur], 1) for c in range(4)] +
                     [(qT[:, 4 + c, :], qT_b[4 + c], 2) for c in range(4)] +
                     [(kB[:, c, cur, :], kB_b[c][cur], 3) for c in range(2)])
            pend_qk = []

            def qk_tail(ps, ps_b, sq, sq_b, dst, dst_b, gi):
                ms, ms_b = self.palloc()
                self.mm(ms, blk64[:], sq[:], True, True, [sq_b], [ms_b])
                (r1, r1_b), (r2, r2_b) = f32r.get(), f32r.get()
                self.act(r1[:, 0:TT], ms, AF.Ln, [ms_b], [r1_b], bias=EPS)
                self.pfree((ms, ms_b))
                self.act(r2[:, 0:TT], r1[:, 0:TT], AF.Exp, [r1_b], [r2_b], scale=-0.5)
                self.stt("dve", dst, ps, qkg[:, gi:gi + 1], r2[:, 0:TT], ALU.mult, ALU.mult,
                         [ps_b, r2_b], [dst_b])
                self.pfree((ps, ps_b))

            for pi in range(4):
                ncols = 512 if pi < 3 else 256
                pv, pb = self.panel_in(wA_F, pi * 512, ncols)
                for j in range(ncols // 128):
                    dst, dst_b, gi = dests[pi * 4 + j]
                    ps, ps_b = self.palloc()
                    for kc in range(8):
                        self.mm(ps, pv[:, kc, j * 128:(j + 1) * 128], hT[:, kc, :], kc == 0, kc == 7,
                                [pb, hT_b[kc]], [ps_b])
                    sq, sq_b = bf16r.get()
                    self.act(sq[:], ps, AF.Square, [ps_b], [sq_b])
                    pend_qk.append((ps, ps_b, sq, sq_b, dst, dst_b, gi))
                    if len(pend_qk) > 2:
                        qk_tail(*pend_qk.pop(0))
            while pend_qk:
                qk_tail(*pend_qk.pop(0))
            pv, pb = self.panel_in(wA_V, 0, 512)
            pv2, pb2 = self.panel_in(wA_V, 512, 128)
            for tb in range(4):
                ps, ps_b = self.palloc()
                for kc in range(8):
                    self.mm(ps, hT[:, kc, tb * 128:(tb + 1) * 128], pv[:, kc, :], kc == 0, kc == 7,
                            [pb, hT_b[kc]], [ps_b])
                self.cp("act", vA[:, cur, tb, :], ps, [ps_b], [vA_b[cur][tb]])
                self.pfree((ps, ps_b))
                ps, ps_b = self.palloc()
                for kc in range(8):
                    self.mm(ps[:, 0:128], hT[:, kc, tb * 128:(tb + 1) * 128], pv2[:, kc, :], kc == 0, kc == 7,
                            [pb2, hT_b[kc]], [ps_b])
                self.cp("dve", vB[:, cur, tb, :], ps[:, 0:128], [ps_b], [vB_b[cur][tb]])
                self.pfree((ps, ps_b))

            def attn_pair(cidx, heads):
                po, po_b = self.palloc()
                pd, pd_b = self.palloc()

                def fr(hd, ki):
                    rows = hd["rows"]
                    k_ap, k_b, v_ap, v_b, q0, N, b_ap = hd["kt"][ki]
                    ps, ps_b = self.palloc()
                    self.mm(ps[:, 0:N], k_ap, hd["q"][rows:rows + 64, q0:q0 + N], True, True,
                            [k_b, hd["q_b"]], [ps_b])
                    pT, pT_b = bf16r.get()
                    self.act(pT[:, 0:N], ps[:, 0:N], AF.Exp, [ps_b], [pT_b])
                    self.pfree((ps, ps_b))
                    self.tt("dve", pT[:, 0:N], pT[:, 0:N], b_ap, ALU.mult, [pT_b], [pT_b])
                    return pT, pT_b

                def bk(hd, ki, pT, pT_b):
                    rows = hd["rows"]
                    nk = len(hd["kt"])
                    k_ap, k_b, v_ap, v_b, q0, N, b_ap = hd["kt"][ki]
                    self.mm(po[rows:rows + 64, q0:q0 + N], v_ap, pT[:, 0:N], ki == 0, ki == nk - 1,
                            [v_b, pT_b], [po_b])
                    self.mm(pd[rows:rows + 64, q0:q0 + N], ones_bf[:], pT[:, 0:N], ki == 0, ki == nk - 1,
                            [pT_b], [pd_b])

                pend = []
                for hd in heads:
                    for ki in range(len(hd["kt"])):
                        pend.append((hd, ki) + fr(hd, ki))
                        if len(pend) > 3:
                            bk(*pend.pop(0))
                while pend:
                    bk(*pend.pop(0))
                rec, rec_b = f32r.get()
                if cidx >= 4:
                    self.act(rec[:, 0:TT], pd, AF.Ln, [pd_b], [rec_b], bias=esink[:, cidx - 4:cidx - 3])
                else:
                    self.act(rec[:, 0:TT], pd, AF.Ln, [pd_b], [rec_b])
                self.act(rec[:, 0:TT], rec[:, 0:TT], AF.Exp, [rec_b], [rec_b], scale=-1.0)
                self.pfree((pd, pd_b))
                self.tt("dve", big[:, cidx, :], po, rec[:, 0:TT], ALU.mult, [po_b, rec_b], [big_b[cidx]])
                self.pfree((po, po_b))

            for c in range(4):
                heads = []
                for half in range(2):
                    h = 2 * c + half
                    rows = 64 * half
                    kt = []
                    for j in range(8):
                        if T == 0 and j < 4:
                            continue
                        slot = prev if j < 4 else cur
                        jj = j % 4
                        i0, i1 = max(0, 2 * j - 8), min(7, 2 * j + 1)
                        N = 64 * (i1 - i0 + 1)
                        d0 = 8 + i0 - 2 * j
                        kt.append((kA[rows:rows + 64, c, slot, jj * 128:(jj + 1) * 128], kA_b[c][slot],
                                   vA[:, slot, jj, h * 64:(h + 1) * 64], vA_b[slot][jj],
                                   64 * i0, N, BTA[:, h, 64 * d0:64 * d0 + N]))
                    heads.append({"rows": rows, "kt": kt, "q": qT[:, c, :], "q_b": qT_b[c]})
                attn_pair(c, heads)
            for c in range(4):
                heads = []
                kv = c // 2
                for half in range(2):
                    hq = 2 * c + half
                    rows = 64 * half
                    kt = []
                    for j in range(-1, 4):
                        if T == 0 and j < 0:
                            continue
                        slot = prev if j < 0 else cur
                        jj = 3 if j < 0 else j
                        i0, i1 = max(0, 2 * j), min(7, 2 * j + 3)
                        N = 64 * (i1 - i0 + 1)
                        d0 = i0 - 2 * j
                        kt.append((kB[rows:rows + 64, kv, slot, jj * 128:(jj + 1) * 128], kB_b[kv][slot],
                                   vB[:, slot, jj, kv * 64:(kv + 1) * 64], vB_b[slot][jj],
                                   64 * i0, N, BTB[:, hq, 64 * d0:64 * d0 + N]))
                    heads.append({"rows": rows, "kt": kt, "q": qT[:, 4 + c, :], "q_b": qT_b[4 + c]})
                attn_pair(4 + c, heads)
            out_proj(wA_O, 8, big_b)

            def ffn(l):
                rmsnorm(16 + 8 * l)
                for pi in range(NFC // 2):
                    pv, pb = self.panel_in(wF_in[l], pi * 512, 512)
                    pss = []
                    for j in range(4):
                        ps, ps_b = self.palloc()
                        for kc in range(8):
                            self.mm(ps, pv[:, kc, j * 128:(j + 1) * 128], hT[:, kc, :], kc == 0, kc == 7,
                                    [pb, hT_b[kc]], [ps_b])
                        pss.append((ps, ps_b))
                    for jj in range(2):
                        fc = 2 * pi + jj
                        pg, pg_b = pss[jj]
                        pu, pu_b = pss[2 + jj]
                        gb, gb_b = f32r.get()
                        tl_b = ftail_b[l][fc]
                        self.cp("pool", gb[:, 0:2], ftail[:, l, fc, :], [tl_b], [gb_b])
                        self.cp("act", gb[:, 2:2 + TT], pg, [pg_b], [gb_b])
                        self.pfree((pg, pg_b))
                        self.cp("pool", ftail[:, l, fc, :], gb[:, TT:TT + 2], [gb_b], [tl_b])
                        t1, t1_b = f32r.get()
                        self.act(t1[:, 0:TT], gb[:, 0:TT], AF.Identity, [gb_b], [t1_b],
                                 scale=fcw[:, l, fc, 0:1], bias=fcb[:, l, fc:fc + 1])
                        self.stt("dve", t1[:, 0:TT], gb[:, 1:1 + TT], fcw[:, l, fc, 1:2], t1[:, 0:TT],
                                 ALU.mult, ALU.add, [gb_b, t1_b], [t1_b])
                        self.stt("dve", t1[:, 0:TT], gb[:, 2:2 + TT], fcw[:, l, fc, 2:3], t1[:, 0:TT],
                                 ALU.mult, ALU.add, [gb_b, t1_b], [t1_b])
                        self.act(t1[:, 0:TT], t1[:, 0:TT], AF.Silu, [t1_b], [t1_b])
                        self.tt("dve", big[:, fc, :], t1[:, 0:TT], pu, ALU.mult, [t1_b, pu_b], [big_b[fc]])
                        self.pfree((pu, pu_b))
                out_proj(wF_out[l], NFC, big_b, after_add=(xio if (l == 1 and CHUNK_IO) else None))

            if self.nstage >= 2:
                ffn(0)

            def ssd():
                rmsnorm(8)
                pvd, pbd = self.panel_in(wS_dt, 0, 32)
                identb = ident
                ps, ps_b = self.palloc()
                for tb in range(4):
                    for kc in range(8):
                        self.mm(ps[:, tb * 32:(tb + 1) * 32], hT[:, kc, tb * 128:(tb + 1) * 128], pvd[:, kc, :],
                                kc == 0, kc == 7, [pbd, hT_b[kc]], [ps_b])
                bc4 = lambda ap: ap.unsqueeze(1).to_broadcast([128, 4, 32])
                v4 = lambda ap: ap.rearrange("p (t h) -> p t h", t=4)
                self.tt("dve", v4(dt4[:]), v4(ps[:, 0:128]), bc4(hpar[:, 0, :]), ALU.add, [ps_b], [dt4_b])
                self.pfree((ps, ps_b))
                self.act(dt4[:], dt4[:], AF.Exp, [dt4_b], [dt4_b])
                self.act(dt4[:], dt4[:], AF.Ln, [dt4_b], [dt4_b], bias=1.0)
                self.tt("dve", v4(dA4[:]), v4(dt4[:]), bc4(hpar[:, 1, :]), ALU.mult, [dt4_b], [dA4_b])
                self.act(lndt4[:], dt4[:], AF.Ln, [dt4_b], [lndt4_b])
                self.cp("dve", lnhi[:], lndt4[:], [lndt4_b], [lnhl_b])
                self.tt("dve", lnlo[:], lndt4[:], lnhi[:], ALU.subtract, [lndt4_b, lnhl_b], [lnhl_b])
                ps, ps_b = self.palloc()
                self.mm(ps[:, 0:128], tri[:], dA4[:], True, True, [dA4_b], [ps_b])
                self.mm(ps[:, 128:256], onesf[:], dA4[:], True, True, [dA4_b], [ps_b])
                self.cp("act", acs4[:], ps[:, 0:128], [ps_b], [acs4_b])
                self.act(E4[:], ps[:, 0:128], AF.Exp, [ps_b], [E4_b])
                self.act(el4[:], ps[:, 128:256], AF.Exp, [ps_b], [el4_b])
                self.cp("act", ww4[:], ps[:, 128:256], [ps_b], [ww4_b])
                self.pfree((ps, ps_b))
                self.tt("dve", ww4[:], ww4[:], acs4[:], ALU.subtract, [ww4_b, acs4_b], [ww4_b])
                self.act(ww4[:], ww4[:], AF.Exp, [ww4_b], [ww4_b])
                self.tt("dve", ww4[:], ww4[:], dt4[:], ALU.mult, [ww4_b, dt4_b], [ww4_b])
                pend_conv = []
                for pi in range(6):
                    pv, pb = self.panel_in(wS_x, pi * 512, 512)
                    for j in range(4):
                        ch = pi * 4 + j
                        ps, ps_b = self.palloc()
                        for kc in range(8):
                            self.mm(ps, pv[:, kc, j * 128:(j + 1) * 128], hT[:, kc, :], kc == 0, kc == 7,
                                    [pb, hT_b[kc]], [ps_b])
                        gb, gb_b = f32r.get()
                        self.cp("pool", gb[:, 0:3], stail[:, ch, :], [stail_b[ch]], [gb_b])
                        self.cp("act", gb[:, 3:3 + TT], ps, [ps_b], [gb_b])
                        self.pfree((ps, ps_b))
                        self.cp("pool", stail[:, ch, :], gb[:, TT:TT + 3], [gb_b], [stail_b[ch]])
                        t1, t1_b = f32r.get()
                        self.act(t1[:, 0:TT], gb[:, 0:TT], AF.Identity, [gb_b], [t1_b],
                                 scale=scw[:, ch, 0:1], bias=scb[:, ch:ch + 1])
                        for k in range(1, 4):
                            self.stt("dve", t1[:, 0:TT], gb[:, k:k + TT], scw[:, ch, k:k + 1], t1[:, 0:TT],
                                     ALU.mult, ALU.add, [gb_b, t1_b], [t1_b])
                        if pend_conv:
                            pt1, pt1_b, pch = pend_conv.pop(0)
                            self.act(big[:, pch, :], pt1[:, 0:TT], AF.Silu, [pt1_b], [big_b[pch]])
                        pend_conv.append((t1, t1_b, ch))
                while pend_conv:
                    pt1, pt1_b, pch = pend_conv.pop(0)
                    self.act(big[:, pch, :], pt1[:, 0:TT], AF.Silu, [pt1_b], [big_b[pch]])
                for pz in range(4):
                    pv, pb = self.panel_in(wS_z, pz * 512, 512)
                    for tb in range(4):
                        ps, ps_b = self.palloc()
                        for kc in range(8):
                            self.mm(ps, hT[:, kc, tb * 128:(tb + 1) * 128], pv[:, kc, :], kc == 0, kc == 7,
                                    [pb, hT_b[kc]], [ps_b])
                        self.act(zs[:, tb, pz * 512:(pz + 1) * 512], ps, AF.Silu, [ps_b], [zs_b[tb]])
                        self.pfree((ps, ps_b))
                blk = {}

                def front0(tb):
                    tc0 = tb * 128
                    st_ = {}
                    xtok, xtok_b = xtok_slots[tb % 2]
                    for hb in range(2):
                        ps, ps_b = self.palloc()
                        psb = ps.bitcast(BF16)
                        for i in range(8):
                            c = hb * 8 + i
                            P.op("pe", lambda e, psb=psb, i=i, c=c, tc0=tc0: e.transpose(psb[:, i * 128:(i + 1) * 128], big[:, c, tc0:tc0 + 128], identb[:]),
                                 reads=[big_b[c]], writes=[ps_b])
                        self.cp("act", xtok[:, hb * 1024:(hb + 1) * 1024], psb[:, 0:1024], [ps_b], xtok_b)
                        self.pfree((ps, ps_b))
                    btok, btok_b = btok_r.get()
                    ps, ps_b = self.palloc()
                    psb = ps.bitcast(BF16)
                    for g in range(4):
                        P.op("pe", lambda e, psb=psb, g=g, tc0=tc0: e.transpose(psb[:, g * 128:(g + 1) * 128], big[:, 16 + g, tc0:tc0 + 128], identb[:]),
                             reads=[big_b[16 + g]], writes=[ps_b])
                    self.cp("act", btok[:], psb[:, 0:512], [ps_b], [btok_b])
                    self.pfree((ps, ps_b))
                    ps, ps_b = self.palloc()
                    for g in range(4):
                        self.mm(ps[:, g * 128:(g + 1) * 128], big[:, 16 + g, tc0:tc0 + 128], big[:, 20 + g, tc0:tc0 + 128],
                                True, True, [big_b[16 + g], big_b[20 + g]], [ps_b])
                    cbm, cbm_b = cbm_r.get()
                    self.tt("dve", cbm[:].rearrange("p (g l) -> p g l", g=4), ps.rearrange("p (g l) -> p g l", g=4),
                            tri[:].unsqueeze(1).to_broadcast([128, 4, 128]), ALU.mult, [ps_b], [cbm_b])
                    self.pfree((ps, ps_b))
                    ss, ss_b = ss_r.get()
                    P.op("pool", lambda e, ss=ss: e.memset(ss[:], 0.0), writes=[ss_b])
                    st_.update(xtok=xtok, xtok_b=xtok_b, btok=btok, btok_b=btok_b, cbm=cbm, cbm_b=cbm_b, ss=ss, ss_b=ss_b, ys=[], mt={})
                    blk[tb] = st_

                def quadR(tb, q):
                    h0 = tb * 32 + q * 4
                    Rq, Rq_b = R_r.get()
                    P.op("dve", lambda e, Rq=Rq, h0=h0: e.tensor_tensor(
                        Rq.rearrange("p (r l) -> p r l", r=4), tri[:].unsqueeze(1).to_broadcast([128, 4, 128]),
                        dA4[:, h0:h0 + 4].unsqueeze(2).to_broadcast([128, 4, 128]), ALU.mult),
                        reads=[dA4_b], writes=Rq_b)
                    ps, ps_b = self.palloc()
                    self.mm(ps, Um[:], Rq, True, False, Rq_b, [ps_b])
                    self.mm(ps.rearrange("p (r l) -> p r l", r=4), ident[:],
                            lnhi[:, h0:h0 + 4].unsqueeze(2).to_broadcast([128, 4, 128]), False, False, [lnhl_b], [ps_b])
                    self.mm(ps.rearrange("p (r l) -> p r l", r=4), ident[:],
                            lnlo[:, h0:h0 + 4].unsqueeze(2).to_broadcast([128, 4, 128]), False, True, [lnhl_b], [ps_b])
                    dec, dec_b = bf16r.get()
                    self.act(dec[:], ps, AF.Exp, [ps_b], [dec_b])
                    self.pfree((ps, ps_b))
                    return dec, dec_b

                def quadM(tb, q, dec, dec_b):
                    st_ = blk[tb]
                    cbm, cbm_b = st_["cbm"], st_["cbm_b"]
                    g = q // 2
                    mt, mt_b = mt_r.get()
                    self.tt("dve", mt[:].rearrange("p (r l) -> p r l", r=4), dec[:].rearrange("p (r l) -> p r l", r=4),
                            cbm[:, g * 128:(g + 1) * 128].unsqueeze(1).to_broadcast([128, 4, 128]), ALU.mult,
                            [dec_b, cbm_b], [mt_b])
                    st_["mt"][q] = (mt, mt_b)

                def xprep(tb, g):
                    st_ = blk[tb]
                    xtok, xtok_b = st_["xtok"], st_["xtok_b"]
                    gc = slice(g * 512, (g + 1) * 512)
                    hs = slice(tb * 32 + g * 8, tb * 32 + (g + 1) * 8)
                    xg3 = xtok[:, gc].rearrange("p (r d) -> p r d", r=8)
                    xw, xw_b = xg_r.get()
                    self.tt("pool", xw[:].rearrange("p (r d) -> p r d", r=8), xg3,
                            ww4[:, hs].unsqueeze(2).to_broadcast([128, 8, 64]), ALU.mult, xtok_b + [ww4_b], [xw_b])
                    st_[g] = (xw, xw_b)
                    self.tt("pool", Sst[:, gc].rearrange("p (r d) -> p r d", r=8), Sst[:, gc].rearrange("p (r d) -> p r d", r=8),
                            el4[:, hs].unsqueeze(2).to_broadcast([128, 8, 64]), ALU.mult, [Sst_b[g], el4_b], [Sst_b[g]])

                def back(tb, g):
                    st_ = blk[tb]
                    tc0 = tb * 128
                    xw, xw_b = st_[g]
                    xtok, xtok_b = st_["xtok"], st_["xtok_b"]
                    mts = [st_["mt"][2 * g], st_["mt"][2 * g + 1]]
                    btok, btok_b, ss, ss_b = st_["btok"], st_["btok_b"], st_["ss"], st_["ss_b"]
                    gc = slice(g * 512, (g + 1) * 512)
                    hs = slice(tb * 32 + g * 8, tb * 32 + (g + 1) * 8)
                    py, py_b = self.palloc()
                    for r in range(8):
                        mt, mt_b = mts[r // 4]
                        self.mm(py[:, r * 64:(r + 1) * 64], mt[:, (r % 4) * 128:(r % 4 + 1) * 128],
                                xtok[:, g * 512 + r * 64:g * 512 + (r + 1) * 64], r == 0, False, [mt_b] + xtok_b, [py_b])
                    for cc in range(4):
                        c = 4 * g + cc
                        self.mm(py[:, cc * 128:(cc + 1) * 128], big[:, c, tc0:tc0 + 128], diagD[:, c, :], False, cc == 3,
                                [big_b[c]], [py_b])
                    pg, pg_b = self.palloc()
                    self.mm(pg, big[:, 20 + g, tc0:tc0 + 128], Sbf[:, gc], True, True, [big_b[20 + g], Sbf_b[g]], [pg_b])
                    pd_, pd_b = self.palloc()
                    self.mm(pd_, btok[:, g * 128:(g + 1) * 128], xw[:], True, True, [btok_b, xw_b], [pd_b])
                    yt, yt_b = f32r.get()
                    self.tt("dve", yt[:, 0:512].rearrange("p (r d) -> p r d", r=8), pg.rearrange("p (r d) -> p r d", r=8),
                            E4[:, hs].unsqueeze(2).to_broadcast([128, 8, 64]), ALU.mult, [pg_b, E4_b], [yt_b])
                    self.pfree((pg, pg_b))
                    self.tt("dve", yt[:, 0:512], yt[:, 0:512], py, ALU.add, [py_b, yt_b], [yt_b])
                    self.pfree((py, py_b))
                    self.tt("pool", yt[:, 0:512], yt[:, 0:512], zs[:, tb, gc], ALU.mult, [yt_b, zs_b[tb]], [yt_b])
                    P.op("act", lambda e, yt=yt, ss=ss, g=g: e.activation(junk[:], yt[:, 0:512], AF.Square, accum_out=ss[:, g:g + 1]),
                         reads=[yt_b], writes=[junk_b, ss_b])
                    st_["ys"].append((yt, yt_b))
                    self.tt("dve", Sst[:, gc], Sst[:, gc], pd_, ALU.add, [Sst_b[g], pd_b], [Sst_b[g]])
                    self.pfree((pd_, pd_b))
                    self.cp("act", Sbf[:, gc], Sst[:, gc], [Sst_b[g]], [Sbf_b[g]])

                def epi(tb):
                    st_ = blk[tb]
                    tc0 = tb * 128
                    ss, ss_b = st_["ss"], st_["ss_b"]
                    yn, yn_b = yn_r.get()
                    self.act(ss[:, 0:4], ss[:, 0:4], AF.Ln, [ss_b], [ss_b], scale=1.0 / 512, bias=EPS)
                    self.act(ss[:, 0:4], ss[:, 0:4], AF.Exp, [ss_b], [ss_b], scale=-0.5)
                    for g in range(4):
                        yt, yt_b = st_["ys"][g]
                        gc = slice(g * 512, (g + 1) * 512)
                        self.stt("dve", yn[:, gc], yt[:, 0:512], ss[:, g:g + 1], normw[:, gc], ALU.mult, ALU.mult,
                                 [yt_b, ss_b], [yn_b])
                    for hb in range(2):
                        ps, ps_b = self.palloc()
                        psb = ps.bitcast(BF16)
                        for i in range(8):
                            c = hb * 8 + i
                            P.op("pe", lambda e, psb=psb, i=i, c=c, yn=yn: e.transpose(psb[:, i * 128:(i + 1) * 128], yn[:, c * 128:(c + 1) * 128], identb[:]),
                                 reads=[yn_b], writes=[ps_b])
                        P.op("act" if hb == 0 else "dve", (lambda e, psb=psb, hb=hb, tc0=tc0: (e.copy if hb == 0 else e.tensor_copy)(
                            big[:, hb * 8:(hb + 1) * 8, tc0:tc0 + 128], psb[:, 0:1024].rearrange("p (c t) -> p c t", c=8))),
                            reads=[ps_b], writes=big_b[hb * 8:(hb + 1) * 8])
                        self.pfree((ps, ps_b))

                LAG = 3
                front0(0)
                for tb in range(4):
                    for g in range(4):
                        xprep(tb, g)
                    if tb < 3:
                        front0(tb + 1)
                    pq = []
                    for q in range(8):
                        pq.append((tb, q) + quadR(tb, q))
                        if len(pq) > LAG:
                            quadM(*pq.pop(0))
                    while pq:
                        quadM(*pq.pop(0))
                    if tb > 0:
                        epi(tb - 1)
                    for g in range(4):
                        back(tb, g)
                epi(3)
                out_proj(wS_O, 16, big_b)

            if self.nstage >= 3:
                if T == 0:
                    for e in ENGS:
                        P.op(e, None, reads=self.consts2)
                ssd()
            if self.nstage >= 4:
                ffn(1)
            if not CHUNK_IO:
                dsto = out_d.rearrange("(c p) t -> p c t", p=128)[:, :, t0:t0 + TT]
                fins.append(P.dma("sp", lambda e, dsto=dsto: e.dma_start(out=dsto, in_=xT[:]), xT_b[0], reads=xT_b))
        P.emit(fins)
        self.st.close()
        return nc


def _prep_shared(inp, nstage):
    f = lambda a: np.ascontiguousarray(np.asarray(a, dtype=np.float32))
    m = {}
    aw = f(inp["attn_w_in"][0])
    qa, ka, va, qb, kb, vb = np.split(aw, [512, 1024, 1536, 2048, 2176], axis=1)
    m["wA_F"] = f(np.concatenate([qa, ka, qb, kb[:, 0:64], kb[:, 0:64], kb[:, 64:128], kb[:, 64:128]], axis=1))
    m["wA_V"] = f(np.concatenate([va, vb], axis=1))
    m["wA_O"] = f(inp["attn_w_out"][0])
    for l in range(2):
        w = f(inp["ffn_w_in"][l])
        g, u = w[:, :DFF], w[:, DFF:]
        cols = []
        for pi in range(NFC // 2):
            cols += [g[:, pi * 256:(pi + 1) * 256], u[:, pi * 256:(pi + 1) * 256]]
        m[f"wF_in{l}"] = f(np.concatenate(cols, axis=1))
        m[f"wF_out{l}"] = f(inp["ffn_w_out"][l])
    gains = np.zeros((128, 32), np.float32)
    for l in range(2):
        gains[:, l * 8:(l + 1) * 8] = f(inp["norm_mix"][l]).reshape(8, 128).T
        gains[:, 16 + l * 8:16 + (l + 1) * 8] = f(inp["norm_ffn"][l]).reshape(8, 128).T
    m["gains"] = gains
    m["qkg"] = f(np.stack([np.tile(f(inp[k][0]), 2) for k in ("q_norm_a", "k_norm_a", "q_norm_b", "k_norm_b")], axis=1))
    m["sinkc"] = f(np.repeat(f(inp["sinks"][0]), 64).reshape(4, 128).T)
    tbl = f(inp["relpos_table"][0])
    kl = np.arange(128)[:, None]
    col = np.arange(640)[None, :]
    rel = col - kl
    idx = np.clip(rel, -256, 256) + 256
    bta = tbl[:, idx]
    dl = (col // 64) * np.ones_like(kl)
    mask = ((kl < 64) & (dl == 9)) | ((kl >= 64) & (dl == 0))
    bta = np.where(mask[None], np.float32(NEG), bta)
    m["BTA"] = f(bta.transpose(1, 0, 2))
    col = np.arange(256)[None, :]
    rel = col - kl
    slopes = (2.0 ** (-8.0 * np.arange(1, 9, dtype=np.float32) / 8)).astype(np.float32)
    btb = -slopes[:, None, None] * np.abs(rel).astype(np.float32)[None]
    dl = (col // 64) * np.ones_like(kl)
    mask = ((kl < 64) & (dl == 3)) | ((kl >= 64) & (dl == 0))
    btb = np.where(mask[None], np.float32(NEG), btb)
    m["BTB"] = f(btb.transpose(1, 0, 2))
    if nstage >= 3:
        sw = f(inp["ssm_w_in"][0])
        m["wS_z"] = f(sw[:, 0:2048])
        m["wS_x"] = f(sw[:, 2048:5120])
        m["wS_dt"] = f(sw[:, 5120:5152])
        m["wS_O"] = f(inp["ssm_w_out"][0])
        m["scw"] = f(f(inp["ssm_conv_w"][0]).reshape(4, 24, 128).transpose(2, 1, 0))
        m["scb"] = f(f(inp["ssm_conv_b"][0]).reshape(24, 128).T)
        hp = np.stack([f(inp["ssm_dt_bias"][0]), f(inp["ssm_a_log"][0]), f(inp["ssm_d"][0])], axis=0)
        m["hpar"] = f(np.broadcast_to(hp[None], (128, 3, 32)))
        m["dcol"] = f(np.repeat(f(inp["ssm_d"][0]), 64).reshape(16, 128).T)
        m["normw"] = f(np.broadcast_to(f(inp["ssm_norm"][0])[None], (128, 2048)))
    m["fcw"] = f(np.stack([f(inp["ffn_conv_w"][l]).reshape(3, NFC, 128).transpose(2, 1, 0) for l in range(2)], axis=1))
    m["fcb"] = f(np.stack([f(inp["ffn_conv_b"][l]).reshape(NFC, 128).T for l in range(2)], axis=1))
    return m


_CACHE = {}


def run(inputs, S=SEQ, nstage=4, ncores=NCORES):
    key = (S, nstage)
    if key not in _CACHE:
        _CACHE[key] = Builder(S, nstage).build()
    nc = _CACHE[key]
    shared = _prep_shared(inputs, nstage)
    x = np.asarray(inputs["x"], dtype=np.float32)
    in_maps = []
    for b in range(ncores):
        m = dict(shared)
        m["xT"] = np.ascontiguousarray(x[b, :S, :].T)
        in_maps.append(m)
    res = run_bass_kernel_spmd(nc, in_maps, core_ids=list(range(ncores)))
    out = np.stack([np.asarray(r["outT"], dtype=np.float32).T for r in res.results], axis=0)
    return out


def kernel(**inputs):
    return run(inputs)
```

```python
from contextlib import ExitStack
import numpy as np
import concourse.bass as bass
import concourse.mybir as mybir
from concourse.bass_utils import run_bass_kernel_spmd

F32 = mybir.dt.float32
BF16 = mybir.dt.bfloat16
ALU = mybir.AluOpType
AF = mybir.ActivationFunctionType

D = 1024
SEQ = 8192
NCORES = 8
TT = 512
EPS = 1e-6
DFF = 2816
NFC = DFF // 128
NEG = -30000.0

SEM_WRAP = 30000
ENGS = ("pe", "act", "dve", "pool", "sp")


class Buf:
    __slots__ = ("name", "w", "rs", "dsem", "dcnt", "excl")

    def __init__(self, name, excl=False):
        self.name = name
        self.w = None
        self.rs = []
        self.dsem = None
        self.dcnt = 0
        self.excl = excl


class Op:
    __slots__ = ("eng", "fn", "waits", "sigkey", "sigord", "signaled", "dma", "dbuf", "sigval")

    def __init__(self, eng, fn):
        self.eng = eng
        self.fn = fn
        self.waits = []
        self.signaled = False
        self.dma = False
        self.dbuf = None
        self.sigval = None


class Prog:
    def __init__(self, nc):
        self.nc = nc
        self.ops = {e: [] for e in ENGS}
        self.wm = {e: {} for e in ENGS}
        self.dma_bufs = []

    def _dep(self, op, prod):
        if prod is None or prod is op:
            return
        if (not prod.dma) and prod.eng == op.eng and op.eng == "pe":
            return
        wm = self.wm[op.eng]
        if wm.get(prod.sigkey, -1) >= prod.sigord:
            return
        wm[prod.sigkey] = prod.sigord
        prod.signaled = True
        op.waits.append(prod)

    def _track(self, o, reads, writes, after):
        for p in after:
            self._dep(o, p)
        for b in reads:
            self._dep(o, b.w)
            if b.excl:
                for r in b.rs:
                    if r.eng != o.eng:
                        self._dep(o, r)
        for b in writes:
            self._dep(o, b.w)
            for r in b.rs:
                self._dep(o, r)
        for b in reads:
            b.rs.append(o)
        for b in writes:
            b.w = o
            b.rs = []

    def op(self, eng, fn, reads=(), writes=(), after=()):
        o = Op(eng, fn)
        lst = self.ops[eng]
        o.sigkey = eng
        o.sigord = len(lst)
        self._track(o, reads, writes, after)
        lst.append(o)
        return o

    def dma(self, eng, fn, dbuf, reads=(), writes=(), after=()):
        o = Op(eng, fn)
        o.dma = True
        o.dbuf = dbuf
        if dbuf.dsem is None:
            dbuf.dsem = True
            self.dma_bufs.append(dbuf)
        dbuf.dcnt += 1
        o.sigkey = ("d", id(dbuf))
        o.sigord = dbuf.dcnt
        o.sigval = 16 * dbuf.dcnt
        self._track(o, reads, writes, after)
        self.ops[eng].append(o)
        return o

    def emit(self, final_waits=()):
        nc = self.nc
        self.op("sp", None, after=final_waits)
        with ExitStack() as st:
            esems = {}
            for e in ENGS:
                n = 0
                for o in self.ops[e]:
                    if o.dma or not o.signaled:
                        continue
                    n += 1
                    o.sigval = n
                nsem = (n + SEM_WRAP - 1) // SEM_WRAP
                esems[e] = [st.enter_context(nc.semaphore(f"s_{e}_{i}")) for i in range(nsem)]
            for i, b in enumerate(self.dma_bufs):
                b.dsem = st.enter_context(nc.semaphore(f"d_{i}"))

            def sem_of(p):
                if p.dma:
                    return p.dbuf.dsem, p.sigval
                n = p.sigval
                return esems[p.eng][(n - 1) // SEM_WRAP], ((n - 1) % SEM_WRAP) + 1

            block = st.enter_context(nc.Block())
            ops = self.ops

            def run(engname):
                def body(eng):
                    for o in ops[engname]:
                        for p in o.waits:
                            s, v = sem_of(p)
                            eng.wait_ge(s, v)
                        if o.fn is None:
                            continue
                        ins = o.fn(eng)
                        if o.dma:
                            ins.then_inc(o.dbuf.dsem, 16)
                        elif o.signaled:
                            s, _ = sem_of(o)
                            ins.then_inc(s, 1)
                return body

            block.tensor(run("pe"))
            block.scalar(run("act"))
            block.vector(run("dve"))
            block.gpsimd(run("pool"))
            block.sync(run("sp"))


class Ring:
    def __init__(self, items):
        self.items = items
        self.i = 0

    def get(self):
        it = self.items[self.i % len(self.items)]
        self.i += 1
        return it


class Builder:
    def __init__(self, S, nstage):
        self.S = S
        self.nstage = nstage
        self.nc = bass.Bass("TRN2", target_bir_lowering=False)
        self.P = Prog(self.nc)
        self.st = ExitStack()
        self.consts = []
        self.wlist = []

    def sb(self, name, shape, dt):
        return self.st.enter_context(self.nc.sbuf_tensor(name, shape, dt))

    def ring(self, name, n, shape, dt):
        return Ring([(self.sb(f"{name}{i}", shape, dt), Buf(f"{name}{i}")) for i in range(n)])

    def din(self, name, shape, dt=F32):
        return self.nc.dram_tensor(name, list(shape), dt, kind="ExternalInput").ap()

    def palloc(self):
        return self.pfree_list.pop(0)

    def pfree(self, b):
        self.pfree_list.append(b)

    def mm(self, out, lhsT, rhs, start, stop, reads, writes):
        self.P.op("pe", lambda e: e.matmul(out, lhsT, rhs, start=start, stop=stop, skip_group_check=True),
                  reads=reads, writes=writes)

    def act(self, out, in_, func, reads, writes, **kw):
        self.P.op("act", lambda e: e.activation(out, in_, func, **kw), reads=reads, writes=writes)

    def tt(self, eng, out, in0, in1, op, reads, writes):
        self.P.op(eng, lambda e: e.tensor_tensor(out, in0, in1, op), reads=reads, writes=writes)

    def ts(self, eng, out, in0, s1, s2, op0, op1, reads, writes):
        if s2 is None:
            self.P.op(eng, lambda e: e.tensor_scalar(out, in0, s1, None, op0), reads=reads, writes=writes)
        else:
            self.P.op(eng, lambda e: e.tensor_scalar(out, in0, s1, s2, op0, op1), reads=reads, writes=writes)

    def stt(self, eng, out, in0, scalar, in1, op0, op1, reads, writes):
        self.P.op(eng, lambda e: e.scalar_tensor_tensor(out, in0, scalar, in1, op0, op1), reads=reads, writes=writes)

    def cp(self, eng, out, in_, reads, writes):
        if eng == "act":
            self.P.op("act", lambda e: e.copy(out, in_), reads=reads, writes=writes)
        else:
            self.P.op(eng, lambda e: e.tensor_copy(out, in_), reads=reads, writes=writes)

    def cload(self, name, shape, dt=F32, via_pool=False, src_shape=None):
        d = self.din(name, shape if src_shape is None else src_shape, F32)
        t = self.sb("c_" + name, shape, dt)
        b = Buf(name)
        eng = "pool" if (via_pool or dt != F32) else "sp"
        self.P.dma(eng, lambda e: e.dma_start(out=t[:], in_=d), b, writes=[b])
        self.consts.append(b)
        return t, b

    def weight(self, name, K, F):
        nc = self.nc
        src = self.din(name, [K, F], F32)
        dst = nc.dram_tensor(name + "_bf", [K, F], BF16, kind="Internal").ap()
        w = {"ap": dst, "src": src, "snap": None, "K": K, "F": F, "ready": False}
        self.wlist.append(w)
        return w

    def emit_casts(self):
        for w in self.wlist:
            K, F, dst, src = w["K"], w["F"], w["ap"], w["src"]
            rows = max(128, (1 << 19) // F // 128 * 128)
            for r0 in range(0, K, rows):
                r1 = min(K, r0 + rows)
                lane = self.lanes.get()[1]
                self.P.dma("pool", lambda e, r0=r0, r1=r1, dst=dst, src=src: e.dma_start(out=dst[r0:r1, :], in_=src[r0:r1, :]),
                           lane, writes=[lane])
            w["snap"] = [l[1].w for l in self.lanes.items if l[1].w is not None]

    def wready(self, w):
        if not w["ready"]:
            self.P.op("sp", None, after=w["snap"])
            w["ready"] = True

    def panel_in(self, w, c0, ncols):
        self.wready(w)
        t, b = self.wslots.get()
        v = t[:, 0:8 * ncols].rearrange("p (k n) -> p k n", k=8)
        src = w["ap"].rearrange("(k p) f -> p k f", p=128)[:, :, c0:c0 + ncols]
        self.P.dma("sp", lambda e: e.dma_start(out=v, in_=src), b, writes=[b])
        return v, b

    def panel_out(self, w, kc0, nk, c0):
        self.wready(w)
        t, b = self.wslots.get()
        v = t[:, 0:nk * 512].rearrange("p (k n) -> p k n", k=nk)
        src = w["ap"].rearrange("(k p) f -> p k f", p=128)[:, kc0:kc0 + nk, c0:c0 + 512]
        self.P.dma("sp", lambda e: e.dma_start(out=v, in_=src), b, writes=[b])
        return v, b

    def build(self):
        nc, P, S = self.nc, self.P, self.S
        NT = S // TT
        self.lanes = Ring([(None, Buf(f"lane{i}")) for i in range(8)])
        xT_d = self.din("xT", [D, S])
        out_d = nc.dram_tensor("outT", [D, S], F32, kind="ExternalOutput").ap()

        pall = self.st.enter_context(nc.psum_tensor("pall", [128, 8, 512], F32))
        self.pbanks = [(pall[:, i, :], Buf(f"ps{i}", excl=True)) for i in range(8)]
        self.pfree_list = list(self.pbanks)

        wA_F = self.weight("wA_F", D, 1792)
        wA_V = self.weight("wA_V", D, 640)
        wA_O = self.weight("wA_O", D, D)
        wF_in = [None, None]
        wF_out = [None, None]
        wF_in[0] = self.weight("wF_in0", D, 2 * DFF)
        wF_out[0] = self.weight("wF_out0", DFF, D)
        if self.nstage >= 3:
            wS_z = self.weight("wS_z", D, 2048)
            wS_x = self.weight("wS_x", D, 3072)
            wS_dt = self.weight("wS_dt", D, 32)
            wS_O = self.weight("wS_O", 2048, D)
            wF_in[1] = self.weight("wF_in1", D, 2 * DFF)
            wF_out[1] = self.weight("wF_out1", DFF, D)

        gains, gains_b = self.cload("gains", [128, 32])
        qkg, qkg_b = self.cload("qkg", [128, 4])
        esink, esink_b = self.cload("sinkc", [128, 4])
        BTA, BTA_b = self.cload("BTA", [128, 8, 640], BF16)
        BTB, BTB_b = self.cload("BTB", [128, 8, 256], BF16)
        fcw, fcw_b = self.cload("fcw", [128, 2, NFC, 3])
        fcb, fcb_b = self.cload("fcb", [128, 2, NFC])
        ident = self.sb("ident", [128, 128], BF16)
        ident_b = Buf("ident")
        onesD = self.sb("onesD", [128, 128], BF16)
        onesD_b = Buf("onesD")
        blk64 = self.sb("blk64", [128, 128], BF16)
        blk64_b = Buf("blk64")
        ones_bf = self.sb("ones_bf", [128, 64], BF16)
        ones_bf_b = Buf("ones_bf")
        P.op("pool", lambda e: e.memset(ident[:], 1.0), writes=[ident_b])
        P.op("pool", lambda e: e.affine_select(ident[:], ident[:], [[1, 128]], ALU.is_equal, 0.0, base=0,
                                               channel_multiplier=-1), reads=[ident_b], writes=[ident_b])
        P.op("pool", lambda e: e.memset(onesD[:], 1.0 / D), writes=[onesD_b])
        P.op("pool", lambda e: e.memset(blk64[:], 0.0), writes=[blk64_b])
        P.op("pool", lambda e: e.memset(blk64[0:64, 0:64], 1.0 / 64), reads=[blk64_b], writes=[blk64_b])
        P.op("pool", lambda e: e.memset(blk64[64:128, 64:128], 1.0 / 64), reads=[blk64_b], writes=[blk64_b])
        P.op("pool", lambda e: e.memset(ones_bf[:], 1.0), writes=[ones_bf_b])
        P.op("dve", lambda e: e.tensor_scalar(qkg[:, 0:1], qkg[:, 0:1], 0.125, None, ALU.mult), reads=[qkg_b], writes=[qkg_b])
        P.op("dve", lambda e: e.tensor_scalar(qkg[:, 2:3], qkg[:, 2:3], 0.125, None, ALU.mult), reads=[qkg_b], writes=[qkg_b])
        P.op("act", lambda e: e.activation(esink[:], esink[:], AF.Exp), reads=[esink_b], writes=[esink_b])
        self.consts += [ident_b, onesD_b, blk64_b, ones_bf_b]
        P.op("act", lambda e: e.activation(BTA[:], BTA[:], AF.Exp), reads=[BTA_b], writes=[BTA_b])
        P.op("act", lambda e: e.activation(BTB[:], BTB[:], AF.Exp), reads=[BTB_b], writes=[BTB_b])

        xT = self.sb("xTs", [128, 8, TT], F32)
        xT_b = [Buf(f"xT{c}") for c in range(8)]
        hT = self.sb("hTs", [128, 8, TT], BF16)
        hT_b = [Buf(f"hT{c}") for c in range(8)]
        xtok_slots = [(hT[:, 4 * i:4 * i + 4, :].rearrange("p c t -> p (c t)"), hT_b[4 * i:4 * i + 4]) for i in range(2)]
        big = self.sb("big", [128, 24, TT], BF16)
        big_b = [Buf(f"big{c}") for c in range(24)]
        qT = self.sb("qTs", [128, 8, TT], BF16)
        qT_b = [Buf(f"qT{c}") for c in range(8)]
        kA = self.sb("kA", [128, 4, 2, TT], BF16)
        kA_b = [[Buf(f"kA{c}_{s}") for s in range(2)] for c in range(4)]
        kB = self.sb("kB", [128, 2, 2, TT], BF16)
        kB_b = [[Buf(f"kB{c}_{s}") for s in range(2)] for c in range(2)]
        vA = self.sb("vA", [128, 2, 4, 512], BF16)
        vA_b = [[Buf(f"vA{s}_{t}") for t in range(4)] for s in range(2)]
        vB = self.sb("vB", [128, 2, 4, 128], BF16)
        vB_b = [[Buf(f"vB{s}_{t}") for t in range(4)] for s in range(2)]
        ftail = self.sb("ftail", [128, 2, NFC, 2], F32)
        ftail_b = [[Buf(f"ftail{l}_{j}") for j in range(NFC)] for l in range(2)]
        P.op("pool", lambda e: e.memset(ftail[:], 0.0), writes=[b for l in ftail_b for b in l])

        self.wslots = self.ring("wslot", 4, [128, 4096], BF16)
        f32r = self.ring("f32r", 6, [128, 516], F32)
        bf16r = self.ring("bf16r", 5, [128, 512], BF16)
        if self.nstage >= 3:
            n_c1 = len(self.consts)
            scw, scw_b = self.cload("scw", [128, 24, 4])
            scb, scb_b = self.cload("scb", [128, 24])
            hpar, hpar_b = self.cload("hpar", [128, 3, 32])
            normw, normw_b = self.cload("normw", [128, 2048], BF16)
            P.op("act", lambda e: e.activation(hpar[:, 1, :], hpar[:, 1, :], AF.Exp), reads=[hpar_b], writes=[hpar_b])
            P.op("dve", lambda e: e.tensor_scalar(hpar[:, 1, :], hpar[:, 1, :], -1.0, None, ALU.mult), reads=[hpar_b], writes=[hpar_b])
            tri = self.sb("tri", [128, 128], F32)
            tri_b = Buf("tri")
            Um = self.sb("Um", [128, 128], F32)
            Um_b = Buf("Um")
            onesf = self.sb("onesf", [128, 128], F32)
            onesf_b = Buf("onesf")
            P.op("pool", lambda e: e.memset(tri[:], 1.0), writes=[tri_b])
            P.op("pool", lambda e: e.affine_select(tri[:], tri[:], [[1, 128]], ALU.is_ge, 0.0, base=0,
                                                   channel_multiplier=-1), reads=[tri_b], writes=[tri_b])
            P.op("pool", lambda e: e.memset(Um[:], 1.0), writes=[Um_b])
            P.op("pool", lambda e: e.affine_select(Um[:], Um[:], [[-1, 128]], ALU.is_gt, 0.0, base=0,
                                                   channel_multiplier=1), reads=[Um_b], writes=[Um_b])
            P.op("pool", lambda e: e.memset(onesf[:], 1.0), writes=[onesf_b])
            dcol, dcol_b = self.cload("dcol", [128, 16])
            diagD = self.sb("diagD", [128, 16, 128], BF16)
            diagD_b = Buf("diagD")
            for c_ in range(16):
                P.op("dve", lambda e, c_=c_: e.tensor_scalar(diagD[:, c_, :], ident[:], dcol[:, c_:c_ + 1], None, ALU.mult),
                     reads=[ident_b, dcol_b], writes=[diagD_b])
            self.consts += [diagD_b]
            self.consts += [tri_b, Um_b, onesf_b, hpar_b]
            self.consts2 = self.consts[n_c1:]
            self.consts = self.consts[:n_c1]
            Sst = self.sb("Sst", [128, 2048], F32)
            Sst_b = [Buf(f"S{g}") for g in range(4)]
            Sbf = self.sb("Sbf", [128, 2048], BF16)
            Sbf_b = [Buf(f"Sbf{g}") for g in range(4)]
            P.op("pool", lambda e: e.memset(Sst[:], 0.0), writes=Sst_b)
            P.op("pool", lambda e: e.memset(Sbf[:], 0.0), writes=Sbf_b)
            stail = self.sb("stail", [128, 24, 3], F32)
            stail_b = [Buf(f"stail{c}") for c in range(24)]
            P.op("pool", lambda e: e.memset(stail[:], 0.0), writes=stail_b)
            zs = self.sb("zs", [128, 4, 2048], BF16)
            zs_b = [Buf(f"zs{t}") for t in range(4)]
            btok_r = self.ring("btok", 2, [128, 512], BF16)
            yn_r = self.ring("yn", 1, [128, 2048], BF16)
            mt_r = self.ring("mt", 8, [128, 512], BF16)
            xg_r = self.ring("xg", 4, [128, 512], BF16)
            R_r = Ring([(qT[:, 2 * i:2 * i + 2, :].rearrange("p c t -> p (c t)").bitcast(F32), [qT_b[2 * i], qT_b[2 * i + 1]])
                        for i in range(4)])
            dt4, dA4, acs4, E4, el4, ww4, lndt4 = [self.sb(n, [128, 128], F32) for n in ("dt4", "dA4", "acs4", "E4", "el4", "ww4", "lndt4")]
            dt4_b, dA4_b, acs4_b, E4_b, el4_b, ww4_b, lndt4_b = [Buf(n) for n in ("dt4", "dA4", "acs4", "E4", "el4", "ww4", "lndt4")]
            lnhi = self.sb("lnhi", [128, 128], BF16)
            lnlo = self.sb("lnlo", [128, 128], BF16)
            lnhl_b = Buf("lnhl")
            junk = self.sb("junk", [128, 512], BF16)
            junk_b = Buf("junk")
            cbm_r = self.ring("cbm", 2, [128, 512], BF16)
            ss_r = self.ring("ss", 3, [128, 8], F32)

        xsrc0 = xT_d.rearrange("(c p) t -> p c t", p=128)[:, :, 0:TT]
        P.dma("sp", lambda e: e.dma_start(out=xT[:], in_=xsrc0), xT_b[0], writes=xT_b)
        self.emit_casts()
        for e in ENGS:
            P.op(e, None, reads=self.consts + [gains_b, qkg_b, esink_b])

        fins = []
        for T in range(NT):
            t0 = T * TT
            cur, prev = T % 2, 1 - (T % 2)
            CHUNK_IO = self.nstage >= 4
            if T > 0 and not CHUNK_IO:
                xsrc = xT_d.rearrange("(c p) t -> p c t", p=128)[:, :, t0:t0 + TT]
                P.dma("sp", lambda e, xsrc=xsrc: e.dma_start(out=xT[:], in_=xsrc), xT_b[0], writes=xT_b)

            def xio(oc, T=T, t0=t0):
                dst = out_d[oc * 128:(oc + 1) * 128, t0:t0 + TT]
                fins.append(P.dma("act", lambda e: e.dma_start(out=dst, in_=xT[:, oc, :]), xT_b[oc], reads=[xT_b[oc]]))
                if T + 1 < NT:
                    for lc in ([oc - 1] if oc % 4 else []) + ([oc] if oc % 4 == 3 else []):
                        src = xT_d[lc * 128:(lc + 1) * 128, t0 + TT:t0 + 2 * TT]
                        P.dma("act", lambda e, lc=lc, src=src: e.dma_start(out=xT[:, lc, :], in_=src), xT_b[lc], writes=[xT_b[lc]])

            def rmsnorm(gcol0):
                for c in range(8):
                    self.act(hT[:, c, :], xT[:, c, :], AF.Square, [xT_b[c]], [hT_b[c]])
                ms, ms_b = self.palloc()
                for c in range(8):
                    self.mm(ms, onesD[:], hT[:, c, :], c == 0, c == 7, [hT_b[c]], [ms_b])
                (r1, r1_b), (r2, r2_b) = f32r.get(), f32r.get()
                self.act(r1[:, 0:TT], ms, AF.Ln, [ms_b], [r1_b], bias=EPS)
                self.pfree((ms, ms_b))
                self.act(r2[:, 0:TT], r1[:, 0:TT], AF.Exp, [r1_b], [r2_b], scale=-0.5)
                for c in range(8):
                    self.stt("dve", hT[:, c, :], xT[:, c, :], gains[:, gcol0 + c:gcol0 + c + 1], r2[:, 0:TT],
                             ALU.mult, ALU.mult, [xT_b[c], r2_b], [hT_b[c]])

            def out_proj(w, nkc, src_b, after_add=None):
                for oh in range(2):
                    pss = [self.palloc() for _ in range(4)]
                    for k0 in range(0, nkc, 8):
                        nk = min(8, nkc - k0)
                        pv, pb = self.panel_out(w, k0, nk, oh * 512)
                        for i in range(nk):
                            kc = k0 + i
                            for o in range(4):
                                self.mm(pss[o][0], pv[:, i, o * 128:(o + 1) * 128], big[:, kc, :], kc == 0,
                                        kc == nkc - 1, [pb, src_b[kc]], [pss[o][1]])
                    for o in range(4):
                        oc = oh * 4 + o
                        self.tt("dve", xT[:, oc, :], xT[:, oc, :], pss[o][0], ALU.add, [pss[o][1], xT_b[oc]], [xT_b[oc]])
                        self.pfree(pss[o])
                        if after_add is not None:
                            after_add(oc)

            rmsnorm(0)
            dests = ([(qT[:, c, :], qT_b[c], 0) for c in range(4)] +
                     [(kA[:, c, cur, :], kA_b[c][cur], 1) for c in range(4)] +
                     [(qT[:, 4 + c, :], qT_b[4 + c], 2) for c in range(4)] +
                     [(kB[:, c, cur, :], kB_b[c][cur], 3) for c in range(2)])
            pend_qk = []

            def qk_tail(ps, ps_b, sq, sq_b, dst, dst_b, gi):
                ms, ms_b = self.palloc()
                self.mm(ms, blk64[:], sq[:], True, True, [sq_b], [ms_b])
                (r1, r1_b), (r2, r2_b) = f32r.get(), f32r.get()
                self.act(r1[:, 0:TT], ms, AF.Ln, [ms_b], [r1_b], bias=EPS)
                self.pfree((ms, ms_b))
                self.act(r2[:, 0:TT], r1[:, 0:TT], AF.Exp, [r1_b], [r2_b], scale=-0.5)
                self.stt("dve", dst, ps, qkg[:, gi:gi + 1], r2[:, 0:TT], ALU.mult, ALU.mult,
                         [ps_b, r2_b], [dst_b])
                self.pfree((ps, ps_b))

            for pi in range(4):
                ncols = 512 if pi < 3 else 256
                pv, pb = self.panel_in(wA_F, pi * 512, ncols)
                for j in range(ncols // 128):
                    dst, dst_b, gi = dests[pi * 4 + j]
                    ps, ps_b = self.palloc()
                    for kc in range(8):
                        self.mm(ps, pv[:, kc, j * 128:(j + 1) * 128], hT[:, kc, :], kc == 0, kc == 7,
                                [pb, hT_b[kc]], [ps_b])
                    sq, sq_b = bf16r.get()
                    self.act(sq[:], ps, AF.Square, [ps_b], [sq_b])
                    pend_qk.append((ps, ps_b, sq, sq_b, dst, dst_b, gi))
                    if len(pend_qk) > 2:
                        qk_tail(*pend_qk.pop(0))
            while pend_qk:
                qk_tail(*pend_qk.pop(0))
            pv, pb = self.panel_in(wA_V, 0, 512)
            pv2, pb2 = self.panel_in(wA_V, 512, 128)
            for tb in range(4):
                ps, ps_b = self.palloc()
                for kc in range(8):
                    self.mm(ps, hT[:, kc, tb * 128:(tb + 1) * 128], pv[:, kc, :], kc == 0, kc == 7,
                            [pb, hT_b[kc]], [ps_b])
                self.cp("act", vA[:, cur, tb, :], ps, [ps_b], [vA_b[cur][tb]])
                self.pfree((ps, ps_b))
                ps, ps_b = self.palloc()
                for kc in range(8):
                    self.mm(ps[:, 0:128], hT[:, kc, tb * 128:(tb + 1) * 128], pv2[:, kc, :], kc == 0, kc == 7,
                            [pb2, hT_b[kc]], [ps_b])
                self.cp("dve", vB[:, cur, tb, :], ps[:, 0:128], [ps_b], [vB_b[cur][tb]])
                self.pfree((ps, ps_b))

            def attn_pair(cidx, heads):
                po, po_b = self.palloc()
                pd, pd_b = self.palloc()

                def fr(hd, ki):
                    rows = hd["rows"]
                    k_ap, k_b, v_ap, v_b, q0, N, b_ap = hd["kt"][ki]
                    ps, ps_b = self.palloc()
                    self.mm(ps[:, 0:N], k_ap, hd["q"][rows:rows + 64, q0:q0 + N], True, True,
                            [k_b, hd["q_b"]], [ps_b])
                    pT, pT_b = bf16r.get()
                    self.act(pT[:, 0:N], ps[:, 0:N], AF.Exp, [ps_b], [pT_b])
                    self.pfree((ps, ps_b))
                    self.tt("dve", pT[:, 0:N], pT[:, 0:N], b_ap, ALU.mult, [pT_b], [pT_b])
                    return pT, pT_b

                def bk(hd, ki, pT, pT_b):
                    rows = hd["rows"]
                    nk = len(hd["kt"])
                    k_ap, k_b, v_ap, v_b, q0, N, b_ap = hd["kt"][ki]
                    self.mm(po[rows:rows + 64, q0:q0 + N], v_ap, pT[:, 0:N], ki == 0, ki == nk - 1,
                            [v_b, pT_b], [po_b])
                    self.mm(pd[rows:rows + 64, q0:q0 + N], ones_bf[:], pT[:, 0:N], ki == 0, ki == nk - 1,
                            [pT_b], [pd_b])

                pend = []
                for hd in heads:
                    for ki in range(len(hd["kt"])):
                        pend.append((hd, ki) + fr(hd, ki))
                        if len(pend) > 4:
                            bk(*pend.pop(0))
                while pend:
                    bk(*pend.pop(0))
                rec, rec_b = f32r.get()
                if cidx >= 4:
                    self.act(rec[:, 0:TT], pd, AF.Ln, [pd_b], [rec_b], bias=esink[:, cidx - 4:cidx - 3])
                else:
                    self.act(rec[:, 0:TT], pd, AF.Ln, [pd_b], [rec_b])
                self.act(rec[:, 0:TT], rec[:, 0:TT], AF.Exp, [rec_b], [rec_b], scale=-1.0)
                self.pfree((pd, pd_b))
                self.tt("dve", big[:, cidx, :], po, rec[:, 0:TT], ALU.mult, [po_b, rec_b], [big_b[cidx]])
                self.pfree((po, po_b))

            for c in range(4):
                heads = []
                for half in range(2):
                    h = 2 * c + half
                    rows = 64 * half
                    kt = []
                    for j in range(8):
                        if T == 0 and j < 4:
                            continue
                        slot = prev if j < 4 else cur
                        jj = j % 4
                        i0, i1 = max(0, 2 * j - 8), min(7, 2 * j + 1)
                        N = 64 * (i1 - i0 + 1)
                        d0 = 8 + i0 - 2 * j
                        kt.append((kA[rows:rows + 64, c, slot, jj * 128:(jj + 1) * 128], kA_b[c][slot],
                                   vA[:, slot, jj, h * 64:(h + 1) * 64], vA_b[slot][jj],
                                   64 * i0, N, BTA[:, h, 64 * d0:64 * d0 + N]))
                    heads.append({"rows": rows, "kt": kt, "q": qT[:, c, :], "q_b": qT_b[c]})
                attn_pair(c, heads)
            for c in range(4):
                heads = []
                kv = c // 2
                for half in range(2):
                    hq = 2 * c + half
                    rows = 64 * half
                    kt = []
                    for j in range(-1, 4):
                        if T == 0 and j < 0:
                            continue
                        slot = prev if j < 0 else cur
                        jj = 3 if j < 0 else j
                        i0, i1 = max(0, 2 * j), min(7, 2 * j + 3)
                        N = 64 * (i1 - i0 + 1)
                        d0 = i0 - 2 * j
                        kt.append((kB[rows:rows + 64, kv, slot, jj * 128:(jj + 1) * 128], kB_b[kv][slot],
                                   vB[:, slot, jj, kv * 64:(kv + 1) * 64], vB_b[slot][jj],
                                   64 * i0, N, BTB[:, hq, 64 * d0:64 * d0 + N]))
                    heads.append({"rows": rows, "kt": kt, "q": qT[:, 4 + c, :], "q_b": qT_b[4 + c]})
                attn_pair(4 + c, heads)
            out_proj(wA_O, 8, big_b)

            def ffn(l):
                rmsnorm(16 + 8 * l)
                for pi in range(NFC // 2):
                    pv, pb = self.panel_in(wF_in[l], pi * 512, 512)
                    pss = []
                    for j in range(4):
                        ps, ps_b = self.palloc()
                        for kc in range(8):
                            self.mm(ps, pv[:, kc, j * 128:(j + 1) * 128], hT[:, kc, :], kc == 0, kc == 7,
                                    [pb, hT_b[kc]], [ps_b])
                        pss.append((ps, ps_b))
                    for jj in range(2):
                        fc = 2 * pi + jj
                        pg, pg_b = pss[jj]
                        pu, pu_b = pss[2 + jj]
                        gb, gb_b = f32r.get()
                        tl_b = ftail_b[l][fc]
                        self.cp("pool", gb[:, 0:2], ftail[:, l, fc, :], [tl_b], [gb_b])
                        self.cp("act", gb[:, 2:2 + TT], pg, [pg_b], [gb_b])
                        self.pfree((pg, pg_b))
                        self.cp("pool", ftail[:, l, fc, :], gb[:, TT:TT + 2], [gb_b], [tl_b])
                        t1, t1_b = f32r.get()
                        self.act(t1[:, 0:TT], gb[:, 0:TT], AF.Identity, [gb_b], [t1_b],
                                 scale=fcw[:, l, fc, 0:1], bias=fcb[:, l, fc:fc + 1])
                        self.stt("dve", t1[:, 0:TT], gb[:, 1:1 + TT], fcw[:, l, fc, 1:2], t1[:, 0:TT],
                                 ALU.mult, ALU.add, [gb_b, t1_b], [t1_b])
                        self.stt("dve", t1[:, 0:TT], gb[:, 2:2 + TT], fcw[:, l, fc, 2:3], t1[:, 0:TT],
                                 ALU.mult, ALU.add, [gb_b, t1_b], [t1_b])
                        self.act(t1[:, 0:TT], t1[:, 0:TT], AF.Silu, [t1_b], [t1_b])
                        self.tt("dve", big[:, fc, :], t1[:, 0:TT], pu, ALU.mult, [t1_b, pu_b], [big_b[fc]])
                        self.pfree((pu, pu_b))
                out_proj(wF_out[l], NFC, big_b, after_add=(xio if (l == 1 and CHUNK_IO) else None))

            if self.nstage >= 2:
                ffn(0)

            def ssd():
                rmsnorm(8)
                pvd, pbd = self.panel_in(wS_dt, 0, 32)
                identb = ident
                ps, ps_b = self.palloc()
                for tb in range(4):
                    for kc in range(8):
                        self.mm(ps[:, tb * 32:(tb + 1) * 32], hT[:, kc, tb * 128:(tb + 1) * 128], pvd[:, kc, :],
                                kc == 0, kc == 7, [pbd, hT_b[kc]], [ps_b])
                bc4 = lambda ap: ap.unsqueeze(1).to_broadcast([128, 4, 32])
                v4 = lambda ap: ap.rearrange("p (t h) -> p t h", t=4)
                self.tt("dve", v4(dt4[:]), v4(ps[:, 0:128]), bc4(hpar[:, 0, :]), ALU.add, [ps_b], [dt4_b])
                self.pfree((ps, ps_b))
                self.act(dt4[:], dt4[:], AF.Exp, [dt4_b], [dt4_b])
                self.act(dt4[:], dt4[:], AF.Ln, [dt4_b], [dt4_b], bias=1.0)
                self.tt("dve", v4(dA4[:]), v4(dt4[:]), bc4(hpar[:, 1, :]), ALU.mult, [dt4_b], [dA4_b])
                self.act(lndt4[:], dt4[:], AF.Ln, [dt4_b], [lndt4_b])
                self.cp("dve", lnhi[:], lndt4[:], [lndt4_b], [lnhl_b])
                self.tt("dve", lnlo[:], lndt4[:], lnhi[:], ALU.subtract, [lndt4_b, lnhl_b], [lnhl_b])
                ps, ps_b = self.palloc()
                self.mm(ps[:, 0:128], tri[:], dA4[:], True, True, [dA4_b], [ps_b])
                self.mm(ps[:, 128:256], onesf[:], dA4[:], True, True, [dA4_b], [ps_b])
                self.cp("act", acs4[:], ps[:, 0:128], [ps_b], [acs4_b])
                self.act(E4[:], ps[:, 0:128], AF.Exp, [ps_b], [E4_b])
                self.act(el4[:], ps[:, 128:256], AF.Exp, [ps_b], [el4_b])
                self.cp("act", ww4[:], ps[:, 128:256], [ps_b], [ww4_b])
                self.pfree((ps, ps_b))
                self.tt("dve", ww4[:], ww4[:], acs4[:], ALU.subtract, [ww4_b, acs4_b], [ww4_b])
                self.act(ww4[:], ww4[:], AF.Exp, [ww4_b], [ww4_b])
                self.tt("dve", ww4[:], ww4[:], dt4[:], ALU.mult, [ww4_b, dt4_b], [ww4_b])
                pend_conv = []
                for pi in range(6):
                    pv, pb = self.panel_in(wS_x, pi * 512, 512)
                    for j in range(4):
                        ch = pi * 4 + j
                        ps, ps_b = self.palloc()
                        for kc in range(8):
                            self.mm(ps, pv[:, kc, j * 128:(j + 1) * 128], hT[:, kc, :], kc == 0, kc == 7,
                                    [pb, hT_b[kc]], [ps_b])
                        gb, gb_b = f32r.get()
                        self.cp("pool", gb[:, 0:3], stail[:, ch, :], [stail_b[ch]], [gb_b])
                        self.cp("act", gb[:, 3:3 + TT], ps, [ps_b], [gb_b])
                        self.pfree((ps, ps_b))
                        self.cp("pool", stail[:, ch, :], gb[:, TT:TT + 3], [gb_b], [stail_b[ch]])
                        t1, t1_b = f32r.get()
                        self.act(t1[:, 0:TT], gb[:, 0:TT], AF.Identity, [gb_b], [t1_b],
                                 scale=scw[:, ch, 0:1], bias=scb[:, ch:ch + 1])
                        for k in range(1, 4):
                            self.stt("dve", t1[:, 0:TT], gb[:, k:k + TT], scw[:, ch, k:k + 1], t1[:, 0:TT],
                                     ALU.mult, ALU.add, [gb_b, t1_b], [t1_b])
                        if pend_conv:
                            pt1, pt1_b, pch = pend_conv.pop(0)
                            self.act(big[:, pch, :], pt1[:, 0:TT], AF.Silu, [pt1_b], [big_b[pch]])
                        pend_conv.append((t1, t1_b, ch))
                while pend_conv:
                    pt1, pt1_b, pch = pend_conv.pop(0)
                    self.act(big[:, pch, :], pt1[:, 0:TT], AF.Silu, [pt1_b], [big_b[pch]])
                for pz in range(4):
                    pv, pb = self.panel_in(wS_z, pz * 512, 512)
                    for tb in range(4):
                        ps, ps_b = self.palloc()
                        for kc in range(8):
                            self.mm(ps, hT[:, kc, tb * 128:(tb + 1) * 128], pv[:, kc, :], kc == 0, kc == 7,
                                    [pb, hT_b[kc]], [ps_b])
                        self.act(zs[:, tb, pz * 512:(pz + 1) * 512], ps, AF.Silu, [ps_b], [zs_b[tb]])
                        self.pfree((ps, ps_b))
                blk = {}

                def front0(tb):
                    tc0 = tb * 128
                    st_ = {}
                    xtok, xtok_b = xtok_slots[tb % 2]
                    for hb in range(2):
                        ps, ps_b = self.palloc()
                        psb = ps.bitcast(BF16)
                        for i in range(8):
                            c = hb * 8 + i
                            P.op("pe", lambda e, psb=psb, i=i, c=c, tc0=tc0: e.transpose(psb[:, i * 128:(i + 1) * 128], big[:, c, tc0:tc0 + 128], identb[:]),
                                 reads=[big_b[c]], writes=[ps_b])
                        self.cp("act", xtok[:, hb * 1024:(hb + 1) * 1024], psb[:, 0:1024], [ps_b], xtok_b)
                        self.pfree((ps, ps_b))
                    btok, btok_b = btok_r.get()
                    ps, ps_b = self.palloc()
                    psb = ps.bitcast(BF16)
                    for g in range(4):
                        P.op("pe", lambda e, psb=psb, g=g, tc0=tc0: e.transpose(psb[:, g * 128:(g + 1) * 128], big[:, 16 + g, tc0:tc0 + 128], identb[:]),
                             reads=[big_b[16 + g]], writes=[ps_b])
                    self.cp("act", btok[:], psb[:, 0:512], [ps_b], [btok_b])
                    self.pfree((ps, ps_b))
                    ps, ps_b = self.palloc()
                    for g in range(4):
                        self.mm(ps[:, g * 128:(g + 1) * 128], big[:, 16 + g, tc0:tc0 + 128], big[:, 20 + g, tc0:tc0 + 128],
                                True, True, [big_b[16 + g], big_b[20 + g]], [ps_b])
                    cbm, cbm_b = cbm_r.get()
                    self.tt("dve", cbm[:].rearrange("p (g l) -> p g l", g=4), ps.rearrange("p (g l) -> p g l", g=4),
                            tri[:].unsqueeze(1).to_broadcast([128, 4, 128]), ALU.mult, [ps_b], [cbm_b])
                    self.pfree((ps, ps_b))
                    ss, ss_b = ss_r.get()
                    P.op("pool", lambda e, ss=ss: e.memset(ss[:], 0.0), writes=[ss_b])
                    st_.update(xtok=xtok, xtok_b=xtok_b, btok=btok, btok_b=btok_b, cbm=cbm, cbm_b=cbm_b, ss=ss, ss_b=ss_b, ys=[], mt={})
                    blk[tb] = st_

                def quadR(tb, q):
                    h0 = tb * 32 + q * 4
                    Rq, Rq_b = R_r.get()
                    P.op("dve", lambda e, Rq=Rq, h0=h0: e.tensor_tensor(
                        Rq.rearrange("p (r l) -> p r l", r=4), tri[:].unsqueeze(1).to_broadcast([128, 4, 128]),
                        dA4[:, h0:h0 + 4].unsqueeze(2).to_broadcast([128, 4, 128]), ALU.mult),
                        reads=[dA4_b], writes=Rq_b)
                    ps, ps_b = self.palloc()
                    self.mm(ps, Um[:], Rq, True, False, Rq_b, [ps_b])
                    self.mm(ps.rearrange("p (r l) -> p r l", r=4), ident[:],
                            lnhi[:, h0:h0 + 4].unsqueeze(2).to_broadcast([128, 4, 128]), False, False, [lnhl_b], [ps_b])
                    self.mm(ps.rearrange("p (r l) -> p r l", r=4), ident[:],
                            lnlo[:, h0:h0 + 4].unsqueeze(2).to_broadcast([128, 4, 128]), False, True, [lnhl_b], [ps_b])
                    dec, dec_b = bf16r.get()
                    self.act(dec[:], ps, AF.Exp, [ps_b], [dec_b])
                    self.pfree((ps, ps_b))
                    return dec, dec_b

                def quadM(tb, q, dec, dec_b):
                    st_ = blk[tb]
                    cbm, cbm_b = st_["cbm"], st_["cbm_b"]
                    g = q // 2
                    mt, mt_b = mt_r.get()
                    self.tt("dve", mt[:].rearrange("p (r l) -> p r l", r=4), dec[:].rearrange("p (r l) -> p r l", r=4),
                            cbm[:, g * 128:(g + 1) * 128].unsqueeze(1).to_broadcast([128, 4, 128]), ALU.mult,
                            [dec_b, cbm_b], [mt_b])
                    st_["mt"][q] = (mt, mt_b)

                def xprep(tb, g):
                    st_ = blk[tb]
                    xtok, xtok_b = st_["xtok"], st_["xtok_b"]
                    gc = slice(g * 512, (g + 1) * 512)
                    hs = slice(tb * 32 + g * 8, tb * 32 + (g + 1) * 8)
                    xg3 = xtok[:, gc].rearrange("p (r d) -> p r d", r=8)
                    xw, xw_b = xg_r.get()
                    self.tt("pool", xw[:].rearrange("p (r d) -> p r d", r=8), xg3,
                            ww4[:, hs].unsqueeze(2).to_broadcast([128, 8, 64]), ALU.mult, xtok_b + [ww4_b], [xw_b])
                    st_[g] = (xw, xw_b)
                    self.tt("pool", Sst[:, gc].rearrange("p (r d) -> p r d", r=8), Sst[:, gc].rearrange("p (r d) -> p r d", r=8),
                            el4[:, hs].unsqueeze(2).to_broadcast([128, 8, 64]), ALU.mult, [Sst_b[g], el4_b], [Sst_b[g]])

                def back(tb, g):
                    st_ = blk[tb]
                    tc0 = tb * 128
                    xw, xw_b = st_[g]
                    xtok, xtok_b = st_["xtok"], st_["xtok_b"]
                    mts = [st_["mt"][2 * g], st_["mt"][2 * g + 1]]
                    btok, btok_b, ss, ss_b = st_["btok"], st_["btok_b"], st_["ss"], st_["ss_b"]
                    gc = slice(g * 512, (g + 1) * 512)
                    hs = slice(tb * 32 + g * 8, tb * 32 + (g + 1) * 8)
                    py, py_b = self.palloc()
                    for r in range(8):
                        mt, mt_b = mts[r // 4]
                        self.mm(py[:, r * 64:(r + 1) * 64], mt[:, (r % 4) * 128:(r % 4 + 1) * 128],
                                xtok[:, g * 512 + r * 64:g * 512 + (r + 1) * 64], r == 0, False, [mt_b] + xtok_b, [py_b])
                    for cc in range(4):
                        c = 4 * g + cc
                        self.mm(py[:, cc * 128:(cc + 1) * 128], big[:, c, tc0:tc0 + 128], diagD[:, c, :], False, cc == 3,
                                [big_b[c]], [py_b])
                    pg, pg_b = self.palloc()
                    self.mm(pg, big[:, 20 + g, tc0:tc0 + 128], Sbf[:, gc], True, True, [big_b[20 + g], Sbf_b[g]], [pg_b])
                    pd_, pd_b = self.palloc()
                    self.mm(pd_, btok[:, g * 128:(g + 1) * 128], xw[:], True, True, [btok_b, xw_b], [pd_b])
                    yt, yt_b = f32r.get()
                    self.tt("dve", yt[:, 0:512].rearrange("p (r d) -> p r d", r=8), pg.rearrange("p (r d) -> p r d", r=8),
                            E4[:, hs].unsqueeze(2).to_broadcast([128, 8, 64]), ALU.mult, [pg_b, E4_b], [yt_b])
                    self.pfree((pg, pg_b))
                    self.tt("dve", yt[:, 0:512], yt[:, 0:512], py, ALU.add, [py_b, yt_b], [yt_b])
                    self.pfree((py, py_b))
                    self.tt("pool", yt[:, 0:512], yt[:, 0:512], zs[:, tb, gc], ALU.mult, [yt_b, zs_b[tb]], [yt_b])
                    P.op("act", lambda e, yt=yt, ss=ss, g=g: e.activation(junk[:], yt[:, 0:512], AF.Square, accum_out=ss[:, g:g + 1]),
                         reads=[yt_b], writes=[junk_b, ss_b])
                    st_["ys"].append((yt, yt_b))
                    self.tt("dve", Sst[:, gc], Sst[:, gc], pd_, ALU.add, [Sst_b[g], pd_b], [Sst_b[g]])
                    self.pfree((pd_, pd_b))
                    self.cp("act", Sbf[:, gc], Sst[:, gc], [Sst_b[g]], [Sbf_b[g]])

                def epi(tb):
                    st_ = blk[tb]
                    tc0 = tb * 128
                    ss, ss_b = st_["ss"], st_["ss_b"]
                    yn, yn_b = yn_r.get()
                    self.act(ss[:, 0:4], ss[:, 0:4], AF.Ln, [ss_b], [ss_b], scale=1.0 / 512, bias=EPS)
                    self.act(ss[:, 0:4], ss[:, 0:4], AF.Exp, [ss_b], [ss_b], scale=-0.5)
                    for g in range(4):
                        yt, yt_b = st_["ys"][g]
                        gc = slice(g * 512, (g + 1) * 512)
                        self.stt("dve", yn[:, gc], yt[:, 0:512], ss[:, g:g + 1], normw[:, gc], ALU.mult, ALU.mult,
                                 [yt_b, ss_b], [yn_b])
                    for hb in range(2):
                        ps, ps_b = self.palloc()
                        psb = ps.bitcast(BF16)
                        for i in range(8):
                            c = hb * 8 + i
                            P.op("pe", lambda e, psb=psb, i=i, c=c, yn=yn: e.transpose(psb[:, i * 128:(i + 1) * 128], yn[:, c * 128:(c + 1) * 128], identb[:]),
                                 reads=[yn_b], writes=[ps_b])
                        P.op("act" if hb == 0 else "dve", (lambda e, psb=psb, hb=hb, tc0=tc0: (e.copy if hb == 0 else e.tensor_copy)(
                            big[:, hb * 8:(hb + 1) * 8, tc0:tc0 + 128], psb[:, 0:1024].rearrange("p (c t) -> p c t", c=8))),
                            reads=[ps_b], writes=big_b[hb * 8:(hb + 1) * 8])
                        self.pfree((ps, ps_b))

                LAG = 4
                front0(0)
                for tb in range(4):
                    for g in range(4):
                        xprep(tb, g)
                    if tb < 3:
                        front0(tb + 1)
                    pq = []
                    for q in range(8):
                        pq.append((tb, q) + quadR(tb, q))
                        if len(pq) > LAG:
                            quadM(*pq.pop(0))
                    while pq:
                        quadM(*pq.pop(0))
                    if tb > 0:
                        epi(tb - 1)
                    for g in range(4):
                        back(tb, g)
                epi(3)
                out_proj(wS_O, 16, big_b)

            if self.nstage >= 3:
                if T == 0:
                    for e in ENGS:
                        P.op(e, None, reads=self.consts2)
                ssd()
            if self.nstage >= 4:
                ffn(1)
            if not CHUNK_IO:
                dsto = out_d.rearrange("(c p) t -> p c t", p=128)[:, :, t0:t0 + TT]
                fins.append(P.dma("sp", lambda e, dsto=dsto: e.dma_start(out=dsto, in_=xT[:]), xT_b[0], reads=xT_b))
        P.emit(fins)
        self.st.close()
        return nc


def _prep_shared(inp, nstage):
    f = lambda a: np.ascontiguousarray(np.asarray(a, dtype=np.float32))
    m = {}
    aw = f(inp["attn_w_in"][0])
    qa, ka, va, qb, kb, vb = np.split(aw, [512, 1024, 1536, 2048, 2176], axis=1)
    m["wA_F"] = f(np.concatenate([qa, ka, qb, kb[:, 0:64], kb[:, 0:64], kb[:, 64:128], kb[:, 64:128]], axis=1))
    m["wA_V"] = f(np.concatenate([va, vb], axis=1))
    m["wA_O"] = f(inp["attn_w_out"][0])
    for l in range(2):
        w = f(inp["ffn_w_in"][l])
        g, u = w[:, :DFF], w[:, DFF:]
        cols = []
        for pi in range(NFC // 2):
            cols += [g[:, pi * 256:(pi + 1) * 256], u[:, pi * 256:(pi + 1) * 256]]
        m[f"wF_in{l}"] = f(np.concatenate(cols, axis=1))
        m[f"wF_out{l}"] = f(inp["ffn_w_out"][l])
    gains = np.zeros((128, 32), np.float32)
    for l in range(2):
        gains[:, l * 8:(l + 1) * 8] = f(inp["norm_mix"][l]).reshape(8, 128).T
        gains[:, 16 + l * 8:16 + (l + 1) * 8] = f(inp["norm_ffn"][l]).reshape(8, 128).T
    m["gains"] = gains
    m["qkg"] = f(np.stack([np.tile(f(inp[k][0]), 2) for k in ("q_norm_a", "k_norm_a", "q_norm_b", "k_norm_b")], axis=1))
    m["sinkc"] = f(np.repeat(f(inp["sinks"][0]), 64).reshape(4, 128).T)
    tbl = f(inp["relpos_table"][0])
    kl = np.arange(128)[:, None]
    col = np.arange(640)[None, :]
    rel = col - kl
    idx = np.clip(rel, -256, 256) + 256
    bta = tbl[:, idx]
    dl = (col // 64) * np.ones_like(kl)
    mask = ((kl < 64) & (dl == 9)) | ((kl >= 64) & (dl == 0))
    bta = np.where(mask[None], np.float32(NEG), bta)
    m["BTA"] = f(bta.transpose(1, 0, 2))
    col = np.arange(256)[None, :]
    rel = col - kl
    slopes = (2.0 ** (-8.0 * np.arange(1, 9, dtype=np.float32) / 8)).astype(np.float32)
    btb = -slopes[:, None, None] * np.abs(rel).astype(np.float32)[None]
    dl = (col // 64) * np.ones_like(kl)
    mask = ((kl < 64) & (dl == 3)) | ((kl >= 64) & (dl == 0))
    btb = np.where(mask[None], np.float32(NEG), btb)
    m["BTB"] = f(btb.transpose(1, 0, 2))
    if nstage >= 3:
        sw = f(inp["ssm_w_in"][0])
        m["wS_z"] = f(sw[:, 0:2048])
        m["wS_x"] = f(sw[:, 2048:5120])
        m["wS_dt"] = f(sw[:, 5120:5152])
        m["wS_O"] = f(inp["ssm_w_out"][0])
        m["scw"] = f(f(inp["ssm_conv_w"][0]).reshape(4, 24, 128).transpose(2, 1, 0))
        m["scb"] = f(f(inp["ssm_conv_b"][0]).reshape(24, 128).T)
        hp = np.stack([f(inp["ssm_dt_bias"][0]), f(inp["ssm_a_log"][0]), f(inp["ssm_d"][0])], axis=0)
        m["hpar"] = f(np.broadcast_to(hp[None], (128, 3, 32)))
        m["dcol"] = f(np.repeat(f(inp["ssm_d"][0]), 64).reshape(16, 128).T)
        m["normw"] = f(np.broadcast_to(f(inp["ssm_norm"][0])[None], (128, 2048)))
    m["fcw"] = f(np.stack([f(inp["ffn_conv_w"][l]).reshape(3, NFC, 128).transpose(2, 1, 0) for l in range(2)], axis=1))
    m["fcb"] = f(np.stack([f(inp["ffn_conv_b"][l]).reshape(NFC, 128).T for l in range(2)], axis=1))
    return m


_CACHE = {}


def run(inputs, S=SEQ, nstage=4, ncores=NCORES):
    key = (S, nstage)
    if key not in _CACHE:
        _CACHE[key] = Builder(S, nstage).build()
    nc = _CACHE[key]
    shared = _prep_shared(inputs, nstage)
    x = np.asarray(inputs["x"], dtype=np.float32)
    in_maps = []
    for b in range(ncores):
        m = dict(shared)
        m["xT"] = np.ascontiguousarray(x[b, :S, :].T)
        in_maps.append(m)
    res = run_bass_kernel_spmd(nc, in_maps, core_ids=list(range(ncores)))
    out = np.stack([np.asarray(r["outT"], dtype=np.float32).T for r in res.results], axis=0)
    return out


def kernel(**inputs):
    return run(inputs)
```
